# Optimizing a Trainium2 kernel written in Bass

```python
import jax, jax.numpy as jnp
from jax import lax
import numpy as np

D_MODEL = 1024
BATCH = 4
SEQ = 4096
DEPTH = 2

N_META = 16
N_A_LAYERS = DEPTH // 2
N_B_LAYERS = DEPTH - N_A_LAYERS
POOL_WINDOWS = (2, 4, 8, 16)
N_POOL_GROUPS = len(POOL_WINDOWS)
POOL_GROUP_DIM = D_MODEL // N_POOL_GROUPS
N_HEADS = 16
HEAD_DIM = D_MODEL // N_HEADS
Q_BLOCK = 128
D_FF = ((8 * D_MODEL // 3 + 127) // 128) * 128
CONV_WIDTH = 3
RMS_EPS = 1e-6

kernel_name = "yoco_pool_stickbreak_convffn"


def rms_norm(x, g):
    xf = x.astype(jnp.float32)
    y = xf * lax.rsqrt(jnp.mean(xf * xf, axis=-1, keepdims=True) + RMS_EPS)
    return (y * g.astype(jnp.float32)).astype(x.dtype)


def multiscale_pool(h, w_groups, scale):
    b, l, d = h.shape
    hf = h.astype(jnp.float32)
    csum = jnp.concatenate([jnp.zeros((b, 1, d), jnp.float32), jnp.cumsum(hf, axis=1)], axis=1)
    hg = hf.reshape(b, l, N_POOL_GROUPS, POOL_GROUP_DIM)
    cg = csum.reshape(b, l + 1, N_POOL_GROUPS, POOL_GROUP_DIM)
    t = jnp.arange(l)
    diffs = []
    for g, w in enumerate(POOL_WINDOWS):
        lo = jnp.maximum(t + 1 - w, 0)
        count = (t + 1 - lo).astype(jnp.float32)
        cgg = cg[:, :, g]
        window_sum = cgg[:, 1:] - cgg[:, lo]
        diffs.append(window_sum / count[None, :, None] - hg[:, :, g])
    diff = jnp.stack(diffs, axis=2).astype(h.dtype)
    y = jnp.einsum('blgc,gcd->blgd', diff, w_groups).reshape(b, l, d)
    return y * scale


def causal_dwconv(u, w, bias):
    l = u.shape[1]
    up = jnp.pad(u, ((0, 0), (CONV_WIDTH - 1, 0), (0, 0)))
    out = bias + w[0] * up[:, 0:l]
    for k in range(1, CONV_WIDTH):
        out = out + w[k] * up[:, k:k + l]
    return out


def conv_ffn(h, w_up, conv_w, conv_b, w_down):
    u = causal_dwconv(h @ w_up, conv_w, conv_b)
    gate, val = jnp.split(u, 2, axis=-1)
    return (jax.nn.silu(gate) * val) @ w_down


def shared_kv(h, kv_norm, w_kv):
    b, l, _ = h.shape
    kv = rms_norm(h, kv_norm) @ w_kv
    k, v = jnp.split(kv, 2, axis=-1)
    k = k.reshape(b, l, N_HEADS, HEAD_DIM).transpose(0, 2, 1, 3)
    v = v.reshape(b, l, N_HEADS, HEAD_DIM).transpose(0, 2, 1, 3)
    return k, v


def stick_breaking_block(q_blk, pos_q, k, v):
    z = jnp.einsum('bhqd,bhsd->bhqs', q_blk, k).astype(jnp.float32) * (HEAD_DIM ** -0.5)
    pos_k = jnp.arange(k.shape[2])
    mask = pos_k[None, :] < pos_q[:, None]
    log_beta = jax.nn.log_sigmoid(z)
    log_1m_beta = jnp.where(mask, jax.nn.log_sigmoid(-z), 0.0)
    later = lax.cumsum(log_1m_beta, axis=3, reverse=True) - log_1m_beta
    a = jnp.where(mask, jnp.exp(log_beta + later), 0.0)
    return jnp.einsum('bhqs,bhsd->bhqd', a.astype(v.dtype), v)


def stick_breaking_attention(h, w_q, k, v, w_o):
    b, l, d = h.shape
    n_real = l - N_META
    n_blk = n_real // Q_BLOCK
    q = (h @ w_q).reshape(b, l, N_HEADS, HEAD_DIM).transpose(0, 2, 1, 3)
    o_meta = stick_breaking_block(q[:, :, :N_META], jnp.arange(N_META),
                                  k[:, :, :N_META], v[:, :, :N_META])
    q_real = q[:, :, N_META:].reshape(b, N_HEADS, n_blk, Q_BLOCK, HEAD_DIM).transpose(2, 0, 1, 3, 4)
    pos_real = (N_META + jnp.arange(n_real)).reshape(n_blk, Q_BLOCK)
    o_real = lax.map(lambda args: stick_breaking_block(args[0], args[1], k, v), (q_real, pos_real))
    o_real = o_real.transpose(1, 2, 0, 3, 4).reshape(b, N_HEADS, n_real, HEAD_DIM)
    o = jnp.concatenate([o_meta, o_real], axis=2).transpose(0, 2, 1, 3).reshape(b, l, d)
    return o @ w_o


def setup_inputs(seed: int = 0) -> dict:
    key = jax.random.key(seed)
    ks = jax.random.split(key, 16)
    f32 = jnp.float32
    nrm = lambda k, shape, s: jax.random.normal(k, shape, f32) * s
    return {
        "x": nrm(ks[0], (BATCH, SEQ, D_MODEL), 1.0),
        "meta_tokens": nrm(ks[1], (N_META, D_MODEL), 1.0),
        "mix_norm": 1.0 + nrm(ks[2], (DEPTH, D_MODEL), 0.05),
        "ffn_norm": 1.0 + nrm(ks[3], (DEPTH, D_MODEL), 0.05),
        "pool_w": nrm(ks[4], (N_A_LAYERS, N_POOL_GROUPS, POOL_GROUP_DIM, POOL_GROUP_DIM), POOL_GROUP_DIM ** -0.5),
        "pool_scale": 1.0 + nrm(ks[5], (N_A_LAYERS, D_MODEL), 0.1),
        "kv_norm": 1.0 + nrm(ks[6], (D_MODEL,), 0.05),
        "w_kv": nrm(ks[7], (D_MODEL, 2 * D_MODEL), D_MODEL ** -0.5),
        "w_q": nrm(ks[8], (N_B_LAYERS, D_MODEL, D_MODEL), D_MODEL ** -0.5),
        "w_o": nrm(ks[9], (N_B_LAYERS, D_MODEL, D_MODEL), D_MODEL ** -0.5),
        "ffn_w_up": nrm(ks[10], (DEPTH, D_MODEL, 2 * D_FF), D_MODEL ** -0.5),
        "ffn_conv_w": nrm(ks[11], (DEPTH, CONV_WIDTH, 2 * D_FF), CONV_WIDTH ** -0.5),
        "ffn_conv_b": nrm(ks[12], (DEPTH, 2 * D_FF), 0.01),
        "ffn_w_down": nrm(ks[13], (DEPTH, D_FF, D_MODEL), D_FF ** -0.5),
        "final_norm": 1.0 + nrm(ks[14], (D_MODEL,), 0.05),
    }


def reference(x, meta_tokens, mix_norm, ffn_norm, pool_w, pool_scale, kv_norm, w_kv,
              w_q, w_o, ffn_w_up, ffn_conv_w, ffn_conv_b, ffn_w_down, final_norm):
    b = x.shape[0]
    meta = jnp.broadcast_to(meta_tokens[None].astype(x.dtype), (b, N_META, D_MODEL))
    h = jnp.concatenate([meta, x], axis=1)
    k = v = None
    for layer in range(DEPTH):
        if layer < N_A_LAYERS:
            h = h + multiscale_pool(rms_norm(h, mix_norm[layer]), pool_w[layer], pool_scale[layer])
        else:
            if layer == N_A_LAYERS:
                k, v = shared_kv(h, kv_norm, w_kv)
            j = layer - N_A_LAYERS
            h = h + stick_breaking_attention(rms_norm(h, mix_norm[layer]), w_q[j], k, v, w_o[j])
        h = h + conv_ffn(rms_norm(h, ffn_norm[layer]), ffn_w_up[layer], ffn_conv_w[layer],
                         ffn_conv_b[layer], ffn_w_down[layer])
    return rms_norm(h, final_norm)[:, N_META:]
```

```python
from contextlib import ExitStack

import numpy as np
import ml_dtypes
import concourse.bass as bass
import concourse.mybir as mybir
from concourse.bass_utils import run_bass_kernel_spmd

F32 = mybir.dt.float32
BF16 = mybir.dt.bfloat16
AF = mybir.ActivationFunctionType
ALU = mybir.AluOpType

D = 1024
NC8 = 8
NMETA = 16
SEQ = 4096
LPAD = 4224
NBLK = 33
TOK = 2176
TILES = [(0, 512), (512, 512), (1024, 512), (1536, 512), (2048, 128)]
DFF = 2816
NJ = 22
GROUPS = [list(range(0, 6)), list(range(6, 12)), list(range(12, 17)), list(range(17, 22))]
EPS = 1e-6
POOLW = (2, 4, 8, 16)
NEG = -30000.0


class Prog:
    ENGS = ("pe", "act", "dve", "pool", "sp")

    def __init__(self, nc):
        self.nc = nc
        self.ops = []
        self.last_w = {}
        self.readers = {}

    def op(self, eng, fn, reads=(), writes=(), dma=None, inc=16):
        oid = len(self.ops)
        deps = set()
        for k in reads:
            w = self.last_w.get(k)
            if w is not None:
                deps.add(w)
        for k in writes:
            w = self.last_w.get(k)
            if w is not None:
                deps.add(w)
            for r in self.readers.get(k, ()):
                deps.add(r)
        deps.discard(oid)
        self.ops.append(dict(eng=eng, fn=fn, deps=deps, dma=dma, inc=inc))
        for k in reads:
            self.readers.setdefault(k, []).append(oid)
        for k in writes:
            self.last_w[k] = oid
            self.readers[k] = []
        return oid

    def build(self, final_keys=(), sem_es=None, prefix="", barrier=False):
        nc = self.nc
        ops = self.ops
        final_ops = set()
        for k in final_keys:
            if k in self.last_w:
                final_ops.add(self.last_w[k])
        needed = set(final_ops)
        if barrier:
            last_of = {}
            for i, o in enumerate(ops):
                if o["dma"] is None:
                    last_of[o["eng"]] = i
            needed.update(last_of.values())
        for o in ops:
            for d in o["deps"]:
                do = ops[d]
                if do["dma"] is None and o["dma"] is None and do["eng"] == o["eng"] == "pe":
                    continue
                needed.add(d)
        chan_names = [("eng", e) for e in self.ENGS]
        for o in ops:
            if o["dma"] is not None and ("dma", o["dma"]) not in chan_names:
                chan_names.append(("dma", o["dma"]))
        counters = {c: 0 for c in chan_names}
        for i, o in enumerate(ops):
            c = ("dma", o["dma"]) if o["dma"] is not None else ("eng", o["eng"])
            o["chan"] = c
            if o["dma"] is not None:
                counters[c] += o["inc"]
                o["tick"] = counters[c]
                o["sig"] = True
            elif i in needed:
                counters[c] += 1
                o["tick"] = counters[c]
                o["sig"] = True
            else:
                o["tick"] = None
                o["sig"] = False
        seen = {e: {} for e in self.ENGS}
        streams = {e: [] for e in self.ENGS}
        for i, o in enumerate(ops):
            e = o["eng"]
            for d in sorted(o["deps"]):
                do = ops[d]
                if not do["sig"]:
                    continue
                if do["dma"] is None and o["dma"] is None and do["eng"] == e == "pe":
                    continue
                c = do["chan"]
                if seen[e].get(c, 0) >= do["tick"]:
                    continue
                seen[e][c] = do["tick"]
                streams[e].append(("wait", c, do["tick"]))
            streams[e].append(("op", i))
        for f in sorted(final_ops):
            do = ops[f]
            c = do["chan"]
            if seen["sp"].get(c, 0) >= do["tick"]:
                continue
            seen["sp"][c] = do["tick"]
            streams["sp"].append(("wait", c, do["tick"]))
        if barrier:
            for e in self.ENGS:
                for c in chan_names:
                    if counters[c] > 0 and seen[e].get(c, 0) < counters[c]:
                        streams[e].append(("wait", c, counters[c]))
        self.n_sems = len(chan_names)
        with ExitStack() as es:
            sems = {}
            for c in chan_names:
                sems[c] = (sem_es or es).enter_context(nc.semaphore(prefix + "s_%s_%s" % c))
            block = es.enter_context(nc.Block())

            def run(engname, eng):
                for s in streams[engname]:
                    if s[0] == "wait":
                        eng.wait_ge(sems[s[1]], s[2])
                    else:
                        o = ops[s[1]]
                        ins = o["fn"](eng)
                        if o["sig"]:
                            ins.then_inc(sems[o["chan"]], o["inc"] if o["dma"] is not None else 1)

            @block.tensor
            def _(eng):
                run("pe", eng)

            @block.scalar
            def _(eng):
                run("act", eng)

            @block.vector
            def _(eng):
                run("dve", eng)

            @block.gpsimd
            def _(eng):
                run("pool", eng)

            @block.sync
            def _(eng):
                run("sp", eng)


class Ctx:
    pass


def alloc_common(nc, es, c):
    pfx = getattr(c, "pfx", "")
    sb = lambda name, shape, dt: es.enter_context(nc.sbuf_tensor(pfx + "sb_" + name, shape, dt))
    if not hasattr(c, "hT"):
        c.hT = sb("hT", [128, NC8, TOK], F32)
    c.nT = sb("nT", [128, NC8, TOK], BF16)
    c.aT = sb("aT", [128, 6, TOK], BF16)
    c.sq = sb("sq", [128, NC8, 512], BF16)
    c.rs = sb("rs", [128, 512], F32)
    c.rstd = sb("rstd", [128, 512], F32)
    c.ones = sb("ones", [128, 128], BF16)
    c.U = [sb("U%d" % i, [128, 2, 516], F32) for i in range(2)]
    c.C = [sb("C%d" % i, [128, 2, 512], F32) for i in range(2)]
    c.wup = [sb("wup%d" % i, [128, 2, NC8, 128], BF16) for i in range(2)]
    c.wd = [sb("wd%d" % i, [128, 6, 128], BF16) for i in range(2)]
    c.wsq = [sb("wsq%d" % i, [128, NC8, 128], BF16) for i in range(2)]
    c.cw = sb("cw", [128, 44, 3], F32)
    c.cb = sb("cb", [128, 44], F32)
    c.gains = sb("gains", [128, 4, NC8], F32)
    if not hasattr(c, "ps"):
        c.ps = es.enter_context(nc.psum_tensor("ps", [128, 8, 512], F32))
    c.eps = sb("epsb", [128, 1], F32)


def emit_consts(P, c):
    P.op("pool", lambda e: e.memset(c.ones[:], 1.0), writes=["ones"])
    P.op("pool", lambda e: e.memset(c.eps[:], EPS), writes=["eps"])
    for i in range(2):
        P.op("pool", lambda e, i=i: e.memset(c.U[i][:, :, 0:2], 0.0), writes=[("U", i)])


def emit_rstd(P, c, ti):
    t0, n = TILES[ti]
    P.op("act", lambda e: e.activation(out=c.sq[:, :, 0:n], in_=c.hT[:, :, t0:t0 + n], func=AF.Square),
         reads=[("h", ti)], writes=["sq"])
    for k in range(NC8):
        P.op("pe", lambda e, k=k: e.matmul(c.ps[:, 0, 0:n], lhsT=c.ones[:], rhs=c.sq[:, k, 0:n],
                                          start=(k == 0), stop=(k == NC8 - 1)),
             reads=["sq", "ones"], writes=[("ps", 0)])
    P.op("act", lambda e: e.activation(out=c.rs[:, 0:n], in_=c.ps[:, 0, 0:n], func=AF.Sqrt,
                                       bias=c.eps[:], scale=1.0 / D),
         reads=[("ps", 0), "eps"], writes=["rs"])
    P.op("dve", lambda e: e.reciprocal(out=c.rstd[:, 0:n], in_=c.rs[:, 0:n]), reads=["rs"], writes=["rstd"])


def emit_norm_bf16(P, c, ti, gidx, dst, dkey):
    t0, n = TILES[ti]
    for k in range(NC8):
        P.op("dve", lambda e, k=k: e.scalar_tensor_tensor(
            out=dst[:, k, t0:t0 + n], in0=c.hT[:, k, t0:t0 + n], scalar=c.gains[:, gidx, k:k + 1],
            in1=c.rstd[:, 0:n], op0=ALU.mult, op1=ALU.mult),
            reads=[("h", ti), "rstd", "gains"], writes=[(dkey, ti)])


def emit_ffn(P, c, w_up, w_down):
    wctr = 0
    dctr = 0
    uctr = 0
    for grp in GROUPS:
        for jj, j in enumerate(grp):
            wb = wctr % 2
            wctr += 1
            for half, col in ((0, j * 128), (1, DFF + j * 128)):
                P.op("pool", lambda e, wb=wb, half=half, col=col: e.dma_start(
                    out=c.wup[wb][:, half, :, :],
                    in_=w_up[:, col:col + 128].rearrange("(k p) f -> p k f", p=128)),
                    writes=[("wup", wb)], dma="wup%d" % wb)
            for ti, (t0, n) in enumerate(TILES):
                ub = uctr % 2
                uctr += 1
                for half in range(2):
                    bank = 1 + 2 * ub + half
                    for k in range(NC8):
                        P.op("pe", lambda e, half=half, k=k, bank=bank, wb=wb, t0=t0, n=n: e.matmul(
                            c.ps[:, bank, 0:n], lhsT=c.wup[wb][:, half, k, :], rhs=c.nT[:, k, t0:t0 + n],
                            start=(k == 0), stop=(k == NC8 - 1)),
                            reads=[("wup", wb), ("n", ti)], writes=[("ps", bank)])
                for half in range(2):
                    bank = 1 + 2 * ub + half
                    ch = j if half == 0 else NJ + j
                    P.op("act", lambda e, half=half, bank=bank, ub=ub, n=n: e.activation(
                        out=c.U[ub][:, half, 2:2 + n], in_=c.ps[:, bank, 0:n], func=AF.Identity),
                        reads=[("ps", bank)], writes=[("U", ub)])
                    P.op("act", lambda e, half=half, bank=bank, ub=ub, n=n, ch=ch: e.activation(
                        out=c.C[ub][:, half, 0:n], in_=c.ps[:, bank, 0:n], func=AF.Identity,
                        bias=c.cb[:, ch:ch + 1], scale=c.cw[:, ch, 2:3]),
                        reads=[("ps", bank), "cw"], writes=[("C", ub)])
                for half in range(2):
                    ch = j if half == 0 else NJ + j
                    for sh in (1, 0):
                        P.op("dve", lambda e, half=half, ub=ub, n=n, ch=ch, sh=sh: e.scalar_tensor_tensor(
                            out=c.C[ub][:, half, 0:n], in0=c.U[ub][:, half, sh:sh + n],
                            scalar=c.cw[:, ch, sh:sh + 1], in1=c.C[ub][:, half, 0:n],
                            op0=ALU.mult, op1=ALU.add),
                            reads=[("U", ub), ("C", ub), "cw"], writes=[("C", ub)])
                if ti + 1 < len(TILES):
                    nb = (ub + 1) % 2
                    P.op("pool", lambda e, ub=ub, nb=nb, n=n: e.tensor_copy(
                        out=c.U[nb][:, :, 0:2], in_=c.U[ub][:, :, n:n + 2]),
                        reads=[("U", ub)], writes=[("U", nb)])
                else:
                    nb = (ub + 1) % 2
                    P.op("pool", lambda e, nb=nb: e.memset(c.U[nb][:, :, 0:2], 0.0), writes=[("U", nb)])
                P.op("act", lambda e, ub=ub, n=n: e.activation(
                    out=c.U[ub][:, 0, 2:2 + n], in_=c.C[ub][:, 0, 0:n], func=AF.Silu),
                    reads=[("C", ub)], writes=[("U", ub)])
                P.op("dve", lambda e, ub=ub, n=n, jj=jj, t0=t0: e.tensor_tensor(
                    out=c.aT[:, jj, t0:t0 + n], in0=c.U[ub][:, 0, 2:2 + n], in1=c.C[ub][:, 1, 0:n], op=ALU.mult),
                    reads=[("U", ub), ("C", ub)], writes=[("a", ti)])
        ng = len(grp)
        r0 = grp[0] * 128
        for dc in range(NC8):
            db = dctr % 2
            dctr += 1
            P.op("pool", lambda e, db=db, dc=dc, ng=ng, r0=r0: e.dma_start(
                out=c.wd[db][:, 0:ng, :],
                in_=w_down[r0:r0 + ng * 128, dc * 128:(dc + 1) * 128].rearrange("(j p) d -> p j d", p=128)),
                writes=[("wd", db)], dma="wd%d" % db)
            for ti, (t0, n) in enumerate(TILES):
                bank = 5 + (ti % 2)
                for jj in range(ng):
                    P.op("pe", lambda e, jj=jj, bank=bank, db=db, t0=t0, n=n, ng=ng: e.matmul(
                        c.ps[:, bank, 0:n], lhsT=c.wd[db][:, jj, :], rhs=c.aT[:, jj, t0:t0 + n],
                        start=(jj == 0), stop=(jj == ng - 1)),
                        reads=[("wd", db), ("a", ti)], writes=[("ps", bank)])
                P.op("dve", lambda e, bank=bank, dc=dc, t0=t0, n=n: e.tensor_tensor(
                    out=c.hT[:, dc, t0:t0 + n], in0=c.ps[:, bank, 0:n], in1=c.hT[:, dc, t0:t0 + n], op=ALU.add),
                    reads=[("ps", bank), ("h", ti)], writes=[("h", ti)])


def emit_proj_fm(P, c, w, col0, nchunks, ti_list, consume, srcT=None, skey="n", tag="wsq"):
    srcT = c.nT if srcT is None else srcT
    for oc in range(nchunks):
        wb = c.wsq_ctr % 2
        c.wsq_ctr += 1
        col = col0 + oc * 128
        P.op("pool", lambda e, wb=wb, col=col: e.dma_start(
            out=c.wsq[wb][:], in_=w[:, col:col + 128].rearrange("(k p) f -> p k f", p=128)),
            writes=[("wsq", wb)], dma="wsq%d" % wb)
        for ti in ti_list:
            t0, n = TILES[ti]
            bank = 5 + (c.pbank_ctr % 2)
            c.pbank_ctr += 1
            for k in range(NC8):
                P.op("pe", lambda e, k=k, bank=bank, wb=wb, t0=t0, n=n: e.matmul(
                    c.ps[:, bank, 0:n], lhsT=c.wsq[wb][:, k, :], rhs=srcT[:, k, t0:t0 + n],
                    start=(k == 0), stop=(k == NC8 - 1)),
                    reads=[("wsq", wb), (skey, ti)], writes=[("ps", bank)])
            consume(oc, ti, bank)


def l1_front(nc, es, c, P, io):
    xT, gains_d, pscale_d, poolw_d, cwtab_d = io["xT"], io["gains"], io["pscale"], io["poolw"], io["cwtab"]
    w_up, cw_d, cb_d, w_down = io["w_up"], io["cw"], io["cb"], io["w_down"]
    pfx = getattr(c, "pfx", "")
    if True:
        sb = lambda name, shape, dtp: es.enter_context(nc.sbuf_tensor(pfx + "sb_" + name, shape, dtp))
        nbuf = sb("nbuf", [128, NC8, 16 + 512], F32)
        sA = sb("sA", [128, 16 + 512], F32)
        sB = sb("sB", [128, 16 + 512], F32)
        poolw = sb("poolw", [128, 4, 2, 256], BF16)
        pscale = sb("pscale", [128, NC8], F32)
        cwtab = sb("cwtab", [128, 4, 16], F32)
        tmp16 = sb("tmp16", [128, 16], F32)
        emit_consts(P, c)
        P.op("sp", lambda e: e.dma_start(out=c.gains[:], in_=gains_d), writes=["gains"], dma="c0")
        P.op("sp", lambda e: e.dma_start(out=pscale[:], in_=pscale_d), writes=["pscale"], dma="c1")
        P.op("sp", lambda e: e.dma_start(out=cwtab[:], in_=cwtab_d), writes=["cwtab"], dma="c2")
        P.op("sp", lambda e: e.dma_start(out=c.cw[:], in_=cw_d), writes=["cw"], dma="c3")
        P.op("sp", lambda e: e.dma_start(out=c.cb[:], in_=cb_d), writes=["cw"], dma="c3")
        P.op("pool", lambda e: e.dma_start(out=poolw[:], in_=poolw_d.rearrange("g (k p) d -> p g k d", p=128)),
             writes=["poolw"], dma="c4")
        for ti, (t0, n) in enumerate(TILES):
            P.op("sp", lambda e, t0=t0, n=n: e.dma_start(
                out=c.hT[:, :, t0:t0 + n], in_=xT[:, t0:t0 + n].rearrange("(k p) t -> p k t", p=128)),
                writes=[("h", ti)], dma="x%d" % ti)
        P.op("pool", lambda e: e.memset(nbuf[:, :, 0:16], 0.0), writes=["nbuf"])

        for ti, (t0, n) in enumerate(TILES):
            emit_rstd(P, c, ti)
            for k in range(NC8):
                P.op("dve", lambda e, k=k, t0=t0, n=n: e.scalar_tensor_tensor(
                    out=nbuf[:, k, 16:16 + n], in0=c.hT[:, k, t0:t0 + n], scalar=c.gains[:, 0, k:k + 1],
                    in1=c.rstd[:, 0:n], op0=ALU.mult, op1=ALU.mult),
                    reads=[("h", ti), "rstd", "gains"], writes=["nbuf"])
            for k in range(NC8):
                g = k // 2
                w = POOLW[g]
                src = nbuf[:, k, :]
                cur = None
                lvl = 1
                bufs = [sA, sB]
                bi = 0
                while lvl < w:
                    lo = 16 - (w - 2 * lvl) if (w - 2 * lvl) > 0 else 16
                    dst = bufs[bi]
                    prev = src if cur is None else cur
                    P.op("dve", lambda e, dst=dst, prev=prev, lo=lo, lvl=lvl, n=n: e.tensor_tensor(
                        out=dst[:, lo:16 + n], in0=prev[:, lo:16 + n], in1=prev[:, lo - lvl:16 + n - lvl], op=ALU.add),
                        reads=["nbuf", "sA", "sB"], writes=["sA" if bi == 0 else "sB"])
                    cur = dst
                    bi ^= 1
                    lvl *= 2
                P.op("dve", lambda e, cur=cur, k=k, w=w, t0=t0, n=n: e.scalar_tensor_tensor(
                    out=c.nT[:, k, t0:t0 + n], in0=cur[:, 16:16 + n], scalar=1.0 / w, in1=nbuf[:, k, 16:16 + n],
                    op0=ALU.mult, op1=ALU.subtract),
                    reads=["nbuf", "sA", "sB"], writes=[("n", ti)])
                if ti == 0:
                    P.op("dve", lambda e, cur=cur, g=g: e.tensor_tensor(
                        out=tmp16[:], in0=cur[:, 16:32], in1=cwtab[:, g, :], op=ALU.mult),
                        reads=["sA", "sB", "cwtab"], writes=["tmp16"])
                    P.op("dve", lambda e, k=k: e.tensor_tensor(
                        out=c.nT[:, k, 0:16], in0=tmp16[:], in1=nbuf[:, k, 16:32], op=ALU.subtract),
                        reads=["tmp16", "nbuf"], writes=[("n", ti)])
            if ti + 1 < len(TILES):
                P.op("pool", lambda e, n=n: e.tensor_copy(out=nbuf[:, :, 0:16], in_=nbuf[:, :, n:n + 16]),
                     reads=["nbuf"], writes=["nbuf"])
            for oc in range(NC8):
                g = oc // 2
                bank = 5 + (oc % 2)
                for kk in range(2):
                    P.op("pe", lambda e, g=g, kk=kk, oc=oc, bank=bank, t0=t0, n=n: e.matmul(
                        c.ps[:, bank, 0:n], lhsT=poolw[:, g, kk, (oc % 2) * 128:(oc % 2) * 128 + 128],
                        rhs=c.nT[:, 2 * g + kk, t0:t0 + n], start=(kk == 0), stop=(kk == 1)),
                        reads=["poolw", ("n", ti)], writes=[("ps", bank)])
                P.op("dve", lambda e, oc=oc, bank=bank, t0=t0, n=n: e.scalar_tensor_tensor(
                    out=c.hT[:, oc, t0:t0 + n], in0=c.ps[:, bank, 0:n], scalar=pscale[:, oc:oc + 1],
                    in1=c.hT[:, oc, t0:t0 + n], op0=ALU.mult, op1=ALU.add),
                    reads=[("ps", bank), ("h", ti), "pscale"], writes=[("h", ti)])
        for ti in range(len(TILES)):
            emit_rstd(P, c, ti)
            emit_norm_bf16(P, c, ti, 1, c.nT, "n")
        emit_ffn(P, c, w_up, w_down)


def build_l1():
    nc = bass.Bass("TRN2", target_bir_lowering=False)
    dt = lambda name, shape, dtype, kind: nc.dram_tensor(name, shape, dtype, kind=kind).ap()
    xT = dt("xT", [D, TOK], F32, "ExternalInput")
    gains_d = dt("gains", [128, 4, NC8], F32, "ExternalInput")
    pscale_d = dt("pscale", [128, NC8], F32, "ExternalInput")
    poolw_d = dt("poolw", [4, 256, 256], F32, "ExternalInput")
    cwtab_d = dt("cwtab", [128, 4, 16], F32, "ExternalInput")
    w_up = dt("w_up", [D, 2 * DFF], F32, "ExternalInput")
    cw_d = dt("cw", [128, 44, 3], F32, "ExternalInput")
    cb_d = dt("cb", [128, 44], F32, "ExternalInput")
    w_down = dt("w_down", [DFF, D], F32, "ExternalInput")
    w_q = dt("w_q", [D, D], F32, "ExternalInput")
    w_kv = dt("w_kv", [D, 2 * D], F32, "ExternalInput")
    h0T = dt("h0T", [D, TOK], F32, "ExternalOutput")
    qT = dt("qT", [D, TOK], BF16, "ExternalOutput")
    kT = dt("kT", [D, TOK], BF16, "ExternalOutput")
    vv = dt("v", [TOK, D], BF16, "ExternalOutput")

    with ExitStack() as es:
        c = Ctx()
        alloc_common(nc, es, c)
        c.wsq_ctr = 0
        c.pbank_ctr = 0
        P = Prog(nc)
        io = dict(xT=xT, gains=gains_d, pscale=pscale_d, poolw=poolw_d, cwtab=cwtab_d, w_up=w_up, cw=cw_d, cb=cb_d, w_down=w_down)
        l1_front(nc, es, c, P, io)
        sb = lambda name, shape, dtp: es.enter_context(nc.sbuf_tensor("sb_" + name, shape, dtp))
        wv = sb("wv", [128, NC8, 512], BF16)
        stg = [sb("stg%d" % i, [128, 512], BF16) for i in range(2)]
        vstg = stg
        for ti, (t0, n) in enumerate(TILES):
            P.op("sp", lambda e, t0=t0, n=n: e.dma_start(
                out=h0T[:, t0:t0 + n].rearrange("(k p) t -> p k t", p=128), in_=c.hT[:, :, t0:t0 + n]),
                reads=[("h", ti)], writes=["h0T"], dma="oh")
        sctr = [0]

        def evac_to(dst, scale):
            def consume(oc, ti, bank):
                t0, n = TILES[ti]
                s = sctr[0] % 2
                sctr[0] += 1
                P.op("act", lambda e: e.activation(out=stg[s][:, 0:n], in_=c.ps[:, bank, 0:n], func=AF.Identity, scale=scale),
                     reads=[("ps", bank)], writes=[("stg", s)])
                P.op("sp", lambda e: e.dma_start(out=dst[oc * 128:(oc + 1) * 128, t0:t0 + n], in_=stg[s][:, 0:n]),
                     reads=[("stg", s)], writes=["oq%d" % s], dma="stg%d" % s)
            return consume

        all_t = list(range(len(TILES)))
        for ti in all_t:
            emit_rstd(P, c, ti)
            emit_norm_bf16(P, c, ti, 3, c.nT, "n")
        emit_proj_fm(P, c, w_q, 0, NC8, all_t, evac_to(qT, 0.125))
        for ti in all_t:
            emit_rstd(P, c, ti)
            emit_norm_bf16(P, c, ti, 2, c.nT, "n")
        emit_proj_fm(P, c, w_kv, 0, NC8, all_t, evac_to(kT, 1.0))
        vctr = 0
        for hf in range(2):
            P.op("pool", lambda e, hf=hf: e.dma_start(
                out=wv[:], in_=w_kv[:, D + hf * 512:D + (hf + 1) * 512].rearrange("(k p) f -> p k f", p=128)),
                writes=["wv"], dma="wv")
            for ti, (t0, n) in enumerate(TILES):
                for b0 in range(0, n, 128):
                    s = vctr % 2
                    vctr += 1
                    bank = 5 + s
                    for k in range(NC8):
                        P.op("pe", lambda e, k=k, bank=bank, t0=t0, b0=b0: e.matmul(
                            c.ps[:, bank, :], lhsT=c.nT[:, k, t0 + b0:t0 + b0 + 128], rhs=wv[:, k, :],
                            start=(k == 0), stop=(k == NC8 - 1)),
                            reads=["wv", ("n", ti)], writes=[("ps", bank)])
                    P.op("act", lambda e, bank=bank, s=s: e.activation(
                        out=vstg[s][:], in_=c.ps[:, bank, :], func=AF.Identity),
                        reads=[("ps", bank)], writes=[("stg", s)])
                    P.op("sp", lambda e, s=s, t0=t0, b0=b0, hf=hf: e.dma_start(
                        out=vv[t0 + b0:t0 + b0 + 128, hf * 512:(hf + 1) * 512], in_=vstg[s][:]),
                        reads=[("stg", s)], writes=["oq%d" % s], dma="stg%d" % s)
        P.build(final_keys=["h0T", "oq0", "oq1"])
        c.P = P
    return nc


def l3_body(nc, es, c, P, io, load_o=None):
    h0T, oT, gains_d, w_o, w_up, cw_d, cb_d, w_down, outT = (io.get(k) for k in (
        "h0T", "oT", "gains", "w_o", "w_up", "cw", "cb", "w_down", "outT"))
    emit_consts(P, c)
    P.op("sp", lambda e: e.dma_start(out=c.gains[:], in_=gains_d), writes=["gains"], dma="c0")
    P.op("sp", lambda e: e.dma_start(out=c.cw[:], in_=cw_d), writes=["cw"], dma="c3")
    P.op("sp", lambda e: e.dma_start(out=c.cb[:], in_=cb_d), writes=["cw"], dma="c3")
    if load_o is None:
        for ti, (t0, n) in enumerate(TILES):
            P.op("sp", lambda e, t0=t0, n=n: e.dma_start(
                out=c.hT[:, :, t0:t0 + n], in_=h0T[:, t0:t0 + n].rearrange("(k p) t -> p k t", p=128)),
                writes=[("h", ti)], dma="x%d" % ti)
            P.op("sp", lambda e, t0=t0, n=n: e.dma_start(
                out=c.nT[:, :, t0:t0 + n], in_=oT[:, t0:t0 + n].rearrange("(k p) t -> p k t", p=128)),
                writes=[("n", ti)], dma="o%d" % ti)
    else:
        load_o()
    all_t = list(range(len(TILES)))

    def add_to_h(oc, ti, bank):
        t0, n = TILES[ti]
        P.op("dve", lambda e: e.tensor_tensor(
            out=c.hT[:, oc, t0:t0 + n], in0=c.ps[:, bank, 0:n], in1=c.hT[:, oc, t0:t0 + n], op=ALU.add),
            reads=[("ps", bank), ("h", ti)], writes=[("h", ti)])

    emit_proj_fm(P, c, w_o, 0, NC8, all_t, add_to_h)
    for ti in all_t:
        emit_rstd(P, c, ti)
        emit_norm_bf16(P, c, ti, 0, c.nT, "n")
    emit_ffn(P, c, w_up, w_down)
    octr = 0
    for ti, (t0, n) in enumerate(TILES):
        emit_rstd(P, c, ti)
        for k in range(NC8):
            s = octr % 4
            octr += 1
            stg = c.C[s // 2][:, s % 2, :]
            P.op("dve", lambda e, k=k, stg=stg, t0=t0, n=n: e.scalar_tensor_tensor(
                out=stg[:, 0:n], in0=c.hT[:, k, t0:t0 + n], scalar=c.gains[:, 1, k:k + 1],
                in1=c.rstd[:, 0:n], op0=ALU.mult, op1=ALU.mult),
                reads=[("h", ti), "rstd", "gains"], writes=[("ostg", s)])
            P.op("sp", lambda e, k=k, stg=stg, t0=t0, n=n: e.dma_start(
                out=outT[k * 128:(k + 1) * 128, t0:t0 + n], in_=stg[:, 0:n]),
                reads=[("ostg", s)], writes=["out%d" % s], dma="ostg%d" % s)


def build_l3():
    nc = bass.Bass("TRN2", target_bir_lowering=False)
    dt = lambda name, shape, dtype, kind: nc.dram_tensor(name, shape, dtype, kind=kind).ap()
    h0T = dt("h0T", [D, TOK], F32, "ExternalInput")
    oT = dt("oT", [D, TOK], BF16, "ExternalInput")
    gains_d = dt("gains", [128, 4, NC8], F32, "ExternalInput")
    w_o = dt("w_o", [D, D], F32, "ExternalInput")
    w_up = dt("w_up", [D, 2 * DFF], F32, "ExternalInput")
    cw_d = dt("cw", [128, 44, 3], F32, "ExternalInput")
    cb_d = dt("cb", [128, 44], F32, "ExternalInput")
    w_down = dt("w_down", [DFF, D], F32, "ExternalInput")
    outT = dt("outT", [D, TOK], F32, "ExternalOutput")
    with ExitStack() as es:
        c = Ctx()
        alloc_common(nc, es, c)
        c.wsq_ctr = 0
        c.pbank_ctr = 0
        P = Prog(nc)
        io = dict(h0T=h0T, oT=oT, gains=gains_d, w_o=w_o, w_up=w_up, cw=cw_d, cb=cb_d, w_down=w_down, outT=outT)
        l3_body(nc, es, c, P, io)
        P.build(final_keys=["out0", "out1", "out2", "out3"])
        c.P = P
    return nc


def l3_inputs(inp, h0T_list, oT_list):
    maps = []
    gains = np.zeros((128, 4, NC8), np.float32)
    gains[:, 0] = lay_gain(inp["ffn_norm"][1])
    gains[:, 1] = lay_gain(inp["final_norm"])
    cw = np.ascontiguousarray(inp["ffn_conv_w"][1].T.reshape(44, 128, 3).transpose(1, 0, 2))
    cb = np.ascontiguousarray(inp["ffn_conv_b"][1].reshape(44, 128).T)
    for ci in range(len(h0T_list)):
        maps.append({
            "h0T": h0T_list[ci], "oT": oT_list[ci], "gains": gains,
            "w_o": np.ascontiguousarray(inp["w_o"][0]),
            "w_up": np.ascontiguousarray(inp["ffn_w_up"][1]),
            "cw": cw, "cb": cb,
            "w_down": np.ascontiguousarray(inp["ffn_w_down"][1]),
        })
    return maps


QTILES = [(i * 512, 512) for i in range(8)] + [(4096, 128)]


def l2_body(nc, es, P, ps, qT, kT, vv, cst_d, nmask_d, emit_out, pfx=""):
    sb = lambda name, shape, dtp: es.enter_context(nc.sbuf_tensor(pfx + "sb_" + name, shape, dtp))
    qs = [sb("q%d" % i, [128, LPAD], BF16) for i in range(2)]
    ks = [sb("k%d" % i, [128, LPAD], BF16) for i in range(2)]
    vs = [sb("v%d" % i, [128, NBLK, 128], BF16) for i in range(2)]
    osb = sb("osb", [128, 4, LPAD], BF16)
    E = [sb("E%d" % i, [128, 2, 512], F32) for i in range(2)]
    SP = [sb("SP%d" % i, [128, 2, 512], BF16) for i in range(2)]
    A = [sb("A%d" % i, [128, 2, 512], BF16) for i in range(2)]
    Rb = [sb("R%d" % i, [128, 2, 512], BF16) for i in range(2)]
    cst = sb("cst", [128, 3, 128], BF16)
    nmask = sb("nmask", [128, 4, 512], BF16)
    onec = sb("onec", [128, 1], F32)
    if ps is None:
        ps = es.enter_context(nc.psum_tensor("ps", [128, 8, 512], F32))
    P.op("pool", lambda e: e.memset(onec[:], 1.0), writes=["onec"])
    P.op("sp", lambda e: e.dma_start(out=cst[:], in_=cst_d), writes=["cst"], dma="c0")
    P.op("sp", lambda e: e.dma_start(out=nmask[:], in_=nmask_d), writes=["nmask"], dma="c1")
    units = []
    for hp in range(4):
        for qi, (q0, nq) in enumerate(QTILES):
            kmax = (q0 + nq) // 128 - 1
            kdiag0 = q0 // 128
            for i, kb in enumerate(range(kmax, -1, -1)):
                units.append(dict(hp=hp, qi=qi, q0=q0, nq=nq, kb=kb, i=i, last=(kb == 0),
                                  r=(kb - kdiag0) if kb >= kdiag0 else None))
    loaded = set()

    def ensure_loaded(hp):
        if hp in loaded or hp >= 4:
            return
        loaded.add(hp)
        b = hp % 2
        P.op("sp", lambda e: e.dma_start(out=qs[b][:], in_=qT[hp * 128:(hp + 1) * 128, :]),
             reads=["qT_d0", "qT_d1"], writes=[("q", b)], dma="q%d" % b)
        P.op("sp", lambda e: e.dma_start(out=ks[b][:], in_=kT[hp * 128:(hp + 1) * 128, :]),
             reads=["kT_d0", "kT_d1"], writes=[("k", b)], dma="k%d" % b)
        P.op("sp", lambda e: e.dma_start(
            out=vs[b][:], in_=vv[:, hp * 128:(hp + 1) * 128].rearrange("(k p) f -> p k f", p=128)),
            reads=["v_d0", "v_d1"], writes=[("v", b)], dma="v%d" % b)

    def s1a(u, idx):
        b = u["hp"] % 2
        zb = 2 * (idx % 2)
        q0, nq, kb = u["q0"], u["nq"], u["kb"]
        for h in range(2):
            P.op("pe", lambda e, h=h: e.matmul(
                ps[:, zb + h, 0:nq], lhsT=ks[b][64 * h:64 * h + 64, kb * 128:(kb + 1) * 128],
                rhs=qs[b][64 * h:64 * h + 64, q0:q0 + nq], start=True, stop=(u["r"] is None)),
                reads=[("q", b), ("k", b)], writes=[("z", idx % 2)])
            if u["r"] is not None:
                P.op("pe", lambda e, h=h: e.matmul(
                    ps[:, zb + h, 0:nq], lhsT=cst[:, 2, :], rhs=nmask[:, u["r"], 0:nq], start=False, stop=True),
                    reads=["cst", "nmask"], writes=[("z", idx % 2)])
        P.op("act", lambda e: e.activation(out=E[idx % 2][:, :, 0:nq], in_=ps[:, zb:zb + 2, 0:nq], func=AF.Exp),
             reads=[("z", idx % 2)], writes=[("E", idx % 2)])

    def s1b(u, idx):
        nq = u["nq"]
        P.op("act", lambda e: e.activation(out=SP[idx % 2][:, :, 0:nq], in_=E[idx % 2][:, :, 0:nq],
                                           func=AF.Ln, bias=onec[:]),
             reads=[("E", idx % 2), "onec"], writes=[("SP", idx % 2)])
        if not u["last"]:
            i = u["i"]
            if i == 0:
                P.op("pool", lambda e: e.tensor_copy(out=Rb[1][:, :, 0:nq], in_=SP[idx % 2][:, :, 0:nq]),
                     reads=[("SP", idx % 2)], writes=[("R", 1)])
            else:
                P.op("pool", lambda e: e.tensor_tensor(
                    out=Rb[(i + 1) % 2][:, :, 0:nq], in0=Rb[i % 2][:, :, 0:nq], in1=SP[idx % 2][:, :, 0:nq], op=ALU.add),
                    reads=[("SP", idx % 2), ("R", i % 2)], writes=[("R", (i + 1) % 2)])

    def s2(u, idx):
        zb = 2 * (idx % 2)
        nq, i = u["nq"], u["i"]
        for h in range(2):
            P.op("pe", lambda e, h=h: e.matmul(
                ps[:, zb + h, 0:nq], lhsT=cst[:, 0, :], rhs=SP[idx % 2][:, h, 0:nq], start=False, stop=(i == 0)),
                reads=["cst", ("SP", idx % 2), ("E", idx % 2)], writes=[("z", idx % 2)])
            if i > 0:
                P.op("pe", lambda e, h=h: e.matmul(
                    ps[:, zb + h, 0:nq], lhsT=cst[:, 1, :], rhs=Rb[i % 2][:, h, 0:nq], start=False, stop=True),
                    reads=["cst", ("R", i % 2)], writes=[("z", idx % 2)])
        P.op("act", lambda e: e.activation(out=A[idx % 2][:, :, 0:nq], in_=ps[:, zb:zb + 2, 0:nq], func=AF.Exp),
             reads=[("z", idx % 2)], writes=[("A", idx % 2)])

    def s3(u, idx):
        b = u["hp"] % 2
        ob = 4 + (u["qi"] % 2)
        q0, nq, kb, i, hp = u["q0"], u["nq"], u["kb"], u["i"], u["hp"]
        for h in range(2):
            P.op("pe", lambda e, h=h: e.matmul(
                ps[64 * h:64 * h + 64, ob, 0:nq], lhsT=vs[b][:, kb, 64 * h:64 * h + 64],
                rhs=A[idx % 2][:, h, 0:nq], start=(i == 0), stop=u["last"]),
                reads=[("v", b), ("A", idx % 2)], writes=[("o", ob)])
        if u["last"]:
            P.op("dve", lambda e: e.tensor_copy(out=osb[:, hp, q0:q0 + nq], in_=ps[:, ob, 0:nq]),
                 reads=[("o", ob)], writes=[("osb", hp)])
            if u["qi"] == len(QTILES) - 1:
                emit_out(hp, osb)

    n = len(units)
    ensure_loaded(0)
    for idx in range(n + 2):
        if idx < n:
            u = units[idx]
            ensure_loaded(u["hp"])
            if u["qi"] == 4 and u["i"] == 0:
                ensure_loaded(u["hp"] + 1)
            s1a(u, idx)
        if 0 <= idx - 1 < n:
            s2(units[idx - 1], idx - 1)
        if idx < n:
            s1b(units[idx], idx)
        if 0 <= idx - 2 < n:
            s3(units[idx - 2], idx - 2)


def build_l2():
    nc = bass.Bass("TRN2", target_bir_lowering=False)
    dt = lambda name, shape, dtype, kind: nc.dram_tensor(name, shape, dtype, kind=kind).ap()
    qT = dt("qT", [512, LPAD], BF16, "ExternalInput")
    kT = dt("kT", [512, LPAD], BF16, "ExternalInput")
    vv = dt("v", [LPAD, 512], BF16, "ExternalInput")
    cst_d = dt("cst", [128, 3, 128], BF16, "ExternalInput")
    nmask_d = dt("nmask", [128, 4, 512], BF16, "ExternalInput")
    oT = dt("oT", [512, LPAD], BF16, "ExternalOutput")
    with ExitStack() as es:
        P = Prog(nc)

        def emit_out(hp, osb):
            P.op("sp", lambda e: e.dma_start(out=oT[hp * 128:(hp + 1) * 128, :], in_=osb[:, hp, :]),
                 reads=[("osb", hp)], writes=["oT%d" % hp], dma="oT%d" % hp)

        l2_body(nc, es, P, None, qT, kT, vv, cst_d, nmask_d, emit_out)
        P.build(final_keys=["oT0", "oT1", "oT2", "oT3"])
    return nc


def l2_consts():
    j = np.arange(128)
    ntri = np.where(j[:, None] >= j[None, :], -1.0, 0.0)
    cst = np.stack([ntri, -np.ones((128, 128)), np.eye(128)], axis=1).astype(ml_dtypes.bfloat16)
    t = np.arange(512)
    nmask = np.zeros((128, 4, 512), np.float32)
    for r in range(4):
        nmask[:, r, :] = np.where((128 * r + j)[:, None] >= t[None, :], NEG, 0.0)
    return np.ascontiguousarray(cst), nmask.astype(ml_dtypes.bfloat16)


RG = [[0, 1], [2, 3], [4, 5], [6, 7]]


def build_fused():
    nc = bass.Bass("TRN2", target_bir_lowering=False)
    dt = lambda name, shape, dtype, kind="ExternalInput": nc.dram_tensor(name, shape, dtype, kind=kind).ap()
    io1 = dict(xT=dt("xT", [D, TOK], F32), gains=dt("gains1", [128, 4, NC8], F32), pscale=dt("pscale", [128, NC8], F32),
               poolw=dt("poolw", [4, 256, 256], F32), cwtab=dt("cwtab", [128, 4, 16], F32),
               w_up=dt("w_up0", [D, 2 * DFF], F32), cw=dt("cw0", [128, 44, 3], F32), cb=dt("cb0", [128, 44], F32),
               w_down=dt("w_down0", [DFF, D], F32))
    wqkv_d = [dt("wq_hg", [D, 512], F32), dt("wk_hg", [D, 512], F32), dt("wv_hg", [D, 512], F32)]
    cst_d = dt("cst", [128, 3, 128], BF16)
    nmask_d = dt("nmask", [128, 4, 512], BF16)
    sel_d = dt("sel", [128, 2], F32)
    io3 = dict(gains=dt("gains3", [128, 4, NC8], F32), w_o=dt("w_o", [D, D], F32), w_up=dt("w_up1", [D, 2 * DFF], F32),
               cw=dt("cw1", [128, 44, 3], F32), cb=dt("cb1", [128, 44], F32), w_down=dt("w_down1", [DFF, D], F32),
               outT=dt("outT", [D, TOK], F32, "ExternalOutput"))
    snd = [dt("snd%d" % i, [256, TOK], BF16, "Internal") for i in range(2)]
    rcv = [dt("rcv%d" % i, [512, TOK], BF16, "Internal") for i in range(2)]
    nall = [dt("nall%d" % i, [D, LPAD], BF16, "Internal") for i in range(2)]
    qT_d = dt("qT_d", [512, LPAD], BF16, "Internal")
    kT_d = dt("kT_d", [512, LPAD], BF16, "Internal")
    v_d = dt("v_d", [LPAD, 512], BF16, "Internal")
    osnd = [dt("osnd%d" % i, [128, LPAD], BF16, "Internal") for i in range(2)]
    orcv = [dt("orcv%d" % i, [256, LPAD], BF16, "Internal") for i in range(2)]
    oall = dt("oall", [D, LPAD], BF16, "Internal")
    all_t = list(range(len(TILES)))

    def allgather(P, src, dst, skey, dkey):
        P.op("pool", lambda e: e.collective_compute("AllGather", ALU.bypass, replica_groups=RG,
                                                    ins=[src.opt()], outs=[dst.opt()]),
             reads=[skey], writes=[dkey], dma="cc", inc=1)

    with ExitStack() as outer:
        hT = outer.enter_context(nc.sbuf_tensor("sb_hT", [128, NC8, TOK], F32))
        ps = outer.enter_context(nc.psum_tensor("ps", [128, 8, 512], F32))
        with ExitStack() as es:
            c = Ctx()
            c.pfx, c.hT, c.ps = "p1", hT, ps
            alloc_common(nc, es, c)
            c.wsq_ctr = 0
            c.pbank_ctr = 0
            P = Prog(nc)
            l1_front(nc, es, c, P, io1)
            cc = 0
            for which, gidx in ((0, 3), (1, 2)):
                for ti in all_t:
                    emit_rstd(P, c, ti)
                    emit_norm_bf16(P, c, ti, gidx, c.nT, "n")
                for j in range(4):
                    b = cc % 2
                    cc += 1
                    for ti, (t0, n) in enumerate(TILES):
                        P.op("sp", lambda e, b=b, j=j, t0=t0, n=n: e.dma_start(
                            out=snd[b].rearrange("(k p) t -> p k t", p=128)[:, :, t0:t0 + n],
                            in_=c.nT[:, 2 * j:2 * j + 2, t0:t0 + n]),
                            reads=[("n", ti)], writes=[("snd", b)], dma="snd%d" % b)
                    allgather(P, snd[b], rcv[b], ("snd", b), ("rcv", b))
                    P.op("sp", lambda e, b=b, j=j, which=which: e.dma_start(
                        out=nall[which][256 * j:256 * j + 256, 0:TOK], in_=rcv[b][0:256, :]),
                        reads=[("rcv", b)], writes=["nall_a%d" % b], dma="na%d" % b)
                    P.op("sp", lambda e, b=b, j=j, which=which: e.dma_start(
                        out=nall[which][256 * j:256 * j + 256, TOK:LPAD], in_=rcv[b][256:512, 128:TOK]),
                        reads=[("rcv", b)], writes=["nall_b%d" % b], dma="nb%d" % b)
            P.build(final_keys=[], sem_es=outer, prefix="p1", barrier=True)
        with ExitStack() as es:
            sb = lambda name, shape, dtp: es.enter_context(nc.sbuf_tensor("p2a_" + name, shape, dtp))
            wsb = [sb("w%d" % i, [128, NC8, 512], BF16) for i in range(3)]
            ntile = [sb("nt%d" % i, [128, 2, NC8, 512], BF16) for i in range(2)]
            stg = [sb("stg%d" % i, [128, 512], BF16) for i in range(2)]
            P = Prog(nc)
            for i in range(3):
                P.op("pool", lambda e, i=i: e.dma_start(out=wsb[i][:], in_=wqkv_d[i].rearrange("(k p) f -> p k f", p=128)),
                     writes=[("w", i)], dma="w%d" % i)
            sctr = 0
            bctr = 0
            for qi, (t0, n) in enumerate(QTILES):
                nb = qi % 2
                for which in range(2):
                    P.op("sp", lambda e, nb=nb, which=which, t0=t0, n=n: e.dma_start(
                        out=ntile[nb][:, which, :, 0:n],
                        in_=nall[which][:, t0:t0 + n].rearrange("(k p) t -> p k t", p=128)),
                        writes=[("nt", nb)], dma="nt%d" % nb)
                for which, dst, scale, key in ((0, qT_d, 0.125, "qT_d"), (1, kT_d, 1.0, "kT_d")):
                    for oc in range(4):
                        bank = 6 + (bctr % 2)
                        bctr += 1
                        for k in range(NC8):
                            P.op("pe", lambda e, which=which, oc=oc, k=k, bank=bank, nb=nb, n=n: e.matmul(
                                ps[:, bank, 0:n], lhsT=wsb[which][:, k, oc * 128:(oc + 1) * 128],
                                rhs=ntile[nb][:, which, k, 0:n], start=(k == 0), stop=(k == NC8 - 1)),
                                reads=[("w", which), ("nt", nb)], writes=[("ps", bank)])
                        s_ = sctr % 2
                        sctr += 1
                        P.op("act", lambda e, s_=s_, bank=bank, n=n, scale=scale: e.activation(
                            out=stg[s_][:, 0:n], in_=ps[:, bank, 0:n], func=AF.Identity, scale=scale),
                            reads=[("ps", bank)], writes=[("stg", s_)])
                        P.op("sp", lambda e, s_=s_, dst=dst, oc=oc, t0=t0, n=n: e.dma_start(
                            out=dst[oc * 128:(oc + 1) * 128, t0:t0 + n], in_=stg[s_][:, 0:n]),
                            reads=[("stg", s_)], writes=["%s%d" % (key, s_)], dma="stg%d" % s_)
                for b0 in range(0, n, 128):
                    bank = 6 + (bctr % 2)
                    bctr += 1
                    for k in range(NC8):
                        P.op("pe", lambda e, k=k, bank=bank, nb=nb, b0=b0: e.matmul(
                            ps[:, bank, :], lhsT=ntile[nb][:, 1, k, b0:b0 + 128], rhs=wsb[2][:, k, :],
                            start=(k == 0), stop=(k == NC8 - 1)),
                            reads=[("w", 2), ("nt", nb)], writes=[("ps", bank)])
                    s_ = sctr % 2
                    sctr += 1
                    P.op("act", lambda e, s_=s_, bank=bank: e.activation(
                        out=stg[s_][:], in_=ps[:, bank, :], func=AF.Identity),
                        reads=[("ps", bank)], writes=[("stg", s_)])
                    P.op("sp", lambda e, s_=s_, t0=t0, b0=b0: e.dma_start(
                        out=v_d[t0 + b0:t0 + b0 + 128, :], in_=stg[s_][:]),
                        reads=[("stg", s_)], writes=["v_d%d" % s_], dma="stg%d" % s_)
            P.build(final_keys=[], sem_es=outer, prefix="p2a", barrier=True)
        with ExitStack() as es:
            P = Prog(nc)

            def emit_out(hp, osb):
                b = hp % 2
                P.op("sp", lambda e: e.dma_start(out=osnd[b], in_=osb[:, hp, :]),
                     reads=[("osb", hp)], writes=[("osnd", b)], dma="osnd%d" % b)
                allgather(P, osnd[b], orcv[b], ("osnd", b), ("orcv", b))
                for s_ in range(2):
                    P.op("sp", lambda e, s_=s_: e.dma_start(
                        out=oall[s_ * 512 + hp * 128:s_ * 512 + (hp + 1) * 128, :], in_=orcv[b][s_ * 128:(s_ + 1) * 128, :]),
                        reads=[("orcv", b)], writes=["oall%d_%d" % (b, s_)], dma="oa%d_%d" % (b, s_))

            l2_body(nc, es, P, ps, qT_d, kT_d, v_d, cst_d, nmask_d, emit_out, pfx="p2b")
            P.build(final_keys=[], sem_es=outer, prefix="p2b", barrier=True)
        with ExitStack() as es:
            c = Ctx()
            c.pfx, c.hT, c.ps = "p3", hT, ps
            alloc_common(nc, es, c)
            c.wsq_ctr = 0
            c.pbank_ctr = 0
            sel = es.enter_context(nc.sbuf_tensor("p3_sel", [128, 2], F32))
            P = Prog(nc)

            def load_o():
                P.op("sp", lambda e: e.dma_start(out=sel[:], in_=sel_d), writes=["sel"], dma="sel")
                for ti, (t0, n) in enumerate(TILES):
                    P.op("sp", lambda e, t0=t0, n=n: e.dma_start(
                        out=c.nT[:, :, t0:t0 + n], in_=oall[:, t0:t0 + n].rearrange("(k p) t -> p k t", p=128)),
                        writes=[("n", ti)], dma="o%d" % ti)
                    P.op("sp", lambda e, t0=t0, n=n: e.dma_start(
                        out=c.sq[:, :, 0:n],
                        in_=oall[:, LPAD - TOK + t0:LPAD - TOK + t0 + n].rearrange("(k p) t -> p k t", p=128)),
                        writes=["sq"], dma="osq")
                    P.op("dve", lambda e, t0=t0, n=n: e.tensor_scalar(
                        out=c.nT[:, :, t0:t0 + n], in0=c.nT[:, :, t0:t0 + n], scalar1=sel[:, 0:1], scalar2=None, op0=ALU.mult),
                        reads=[("n", ti), "sel"], writes=[("n", ti)])
                    P.op("dve", lambda e, t0=t0, n=n: e.scalar_tensor_tensor(
                        out=c.nT[:, :, t0:t0 + n], in0=c.sq[:, :, 0:n], scalar=sel[:, 1:2], in1=c.nT[:, :, t0:t0 + n],
                        op0=ALU.mult, op1=ALU.add),
                        reads=["sq", ("n", ti), "sel"], writes=[("n", ti)])

            l3_body(nc, es, c, P, io3, load_o)
            P.build(final_keys=["out0", "out1", "out2", "out3"], sem_es=outer, prefix="p3")
    return nc


def fused_inputs(inp):
    maps = []
    gains1 = np.stack([lay_gain(inp["mix_norm"][0]), lay_gain(inp["ffn_norm"][0]),
                       lay_gain(inp["kv_norm"]), lay_gain(inp["mix_norm"][1])], axis=1)
    gains3 = np.zeros((128, 4, NC8), np.float32)
    gains3[:, 0] = lay_gain(inp["ffn_norm"][1])
    gains3[:, 1] = lay_gain(inp["final_norm"])
    lay_cw = lambda w: np.ascontiguousarray(w.T.reshape(44, 128, 3).transpose(1, 0, 2))
    lay_cb = lambda b: np.ascontiguousarray(b.reshape(44, 128).T)
    cst, nmask = l2_consts()
    ca = np.ascontiguousarray
    for b in range(4):
        hs = pad_seq(inp["x"][b], inp["meta_tokens"])
        for half in range(2):
            t0 = core_tok0(half)
            cwt = np.zeros((128, 4, 16), np.float32)
            for g, w in enumerate(POOLW):
                for t in range(16):
                    cwt[:, g, t] = 1.0 / (min(w, t + 1) if half == 0 else w)
            sel = np.zeros((128, 2), np.float32)
            sel[:, half] = 1.0
            hg = slice(half * 512, (half + 1) * 512)
            maps.append({
                "xT": ca(hs[t0:t0 + TOK].T), "gains1": ca(gains1), "pscale": lay_gain(inp["pool_scale"][0]),
                "poolw": ca(inp["pool_w"][0]), "cwtab": cwt,
                "w_up0": ca(inp["ffn_w_up"][0]), "cw0": lay_cw(inp["ffn_conv_w"][0]), "cb0": lay_cb(inp["ffn_conv_b"][0]),
                "w_down0": ca(inp["ffn_w_down"][0]),
                "wq_hg": ca(inp["w_q"][0][:, hg]), "wk_hg": ca(inp["w_kv"][:, hg]),
                "wv_hg": ca(inp["w_kv"][:, D + half * 512:D + (half + 1) * 512]),
                "cst": cst, "nmask": nmask, "sel": sel,
                "gains3": gains3, "w_o": ca(inp["w_o"][0]),
                "w_up1": ca(inp["ffn_w_up"][1]), "cw1": lay_cw(inp["ffn_conv_w"][1]), "cb1": lay_cb(inp["ffn_conv_b"][1]),
                "w_down1": ca(inp["ffn_w_down"][1]),
            })
    return maps


def kernel_fused(**inputs):
    inp = {k: np.asarray(v) for k, v in inputs.items()}
    r = run_bass_kernel_spmd(_prog("fused", build_fused), fused_inputs(inp), core_ids=list(range(8))).results
    out = np.empty((4, SEQ, D), np.float32)
    na = TOK - NMETA
    for b in range(4):
        out[b, :na] = r[2 * b]["outT"].T[NMETA:TOK]
        out[b, na:] = r[2 * b + 1]["outT"].T[128:128 + SEQ - na]
    return out


def lay_gain(g):
    return np.ascontiguousarray(g.reshape(NC8, 128).T)


def pad_seq(x_b, meta):
    out = np.zeros((LPAD, D), np.float32)
    out[:NMETA] = meta
    out[NMETA:NMETA + SEQ] = x_b
    return out


def core_tok0(half):
    return 0 if half == 0 else LPAD - TOK


def l1_inputs(inp):
    maps = []
    gains = np.stack([lay_gain(inp["mix_norm"][0]), lay_gain(inp["ffn_norm"][0]),
                      lay_gain(inp["kv_norm"]), lay_gain(inp["mix_norm"][1])], axis=1)
    cw = np.ascontiguousarray(inp["ffn_conv_w"][0].T.reshape(44, 128, 3).transpose(1, 0, 2))
    cb = np.ascontiguousarray(inp["ffn_conv_b"][0].reshape(44, 128).T)
    for b in range(4):
        hs = pad_seq(inp["x"][b], inp["meta_tokens"])
        for half in range(2):
            t0 = core_tok0(half)
            cwt = np.zeros((128, 4, 16), np.float32)
            for g, w in enumerate(POOLW):
                for t in range(16):
                    cwt[:, g, t] = 1.0 / (min(w, t + 1) if half == 0 else w)
            maps.append({
                "xT": np.ascontiguousarray(hs[t0:t0 + TOK].T),
                "gains": np.ascontiguousarray(gains),
                "pscale": lay_gain(inp["pool_scale"][0]),
                "poolw": np.ascontiguousarray(inp["pool_w"][0]),
                "cwtab": cwt,
                "w_up": np.ascontiguousarray(inp["ffn_w_up"][0]),
                "cw": cw, "cb": cb,
                "w_down": np.ascontiguousarray(inp["ffn_w_down"][0]),
                "w_q": np.ascontiguousarray(inp["w_q"][0]),
                "w_kv": np.ascontiguousarray(inp["w_kv"]),
            })
    return maps


_PROGS = {}


def _prog(name, builder):
    if name not in _PROGS:
        _PROGS[name] = builder()
    return _PROGS[name]


def kernel_unfused(**inputs):
    inp = {k: np.asarray(v) for k, v in inputs.items()}
    cores = list(range(8))
    bf = ml_dtypes.bfloat16
    r1 = run_bass_kernel_spmd(_prog("l1", build_l1), l1_inputs(inp), core_ids=cores).results
    cst, nmask = l2_consts()
    maps2 = []
    for b in range(4):
        a, bb = r1[2 * b], r1[2 * b + 1]
        qf = np.concatenate([a["qT"], bb["qT"][:, 128:]], axis=1)
        kf = np.concatenate([a["kT"], bb["kT"][:, 128:]], axis=1)
        vf = np.concatenate([a["v"], bb["v"][128:]], axis=0)
        for hg in range(2):
            sl = slice(hg * 512, (hg + 1) * 512)
            maps2.append({"qT": np.ascontiguousarray(qf[sl]), "kT": np.ascontiguousarray(kf[sl]),
                          "v": np.ascontiguousarray(vf[:, sl]), "cst": cst, "nmask": nmask})
    r2 = run_bass_kernel_spmd(_prog("l2", build_l2), maps2, core_ids=cores).results
    h0T_list, oT_list = [], []
    for b in range(4):
        of = np.concatenate([r2[2 * b]["oT"], r2[2 * b + 1]["oT"]], axis=0)
        for half in range(2):
            t0 = core_tok0(half)
            h0T_list.append(r1[2 * b + half]["h0T"])
            oT_list.append(np.ascontiguousarray(of[:, t0:t0 + TOK]))
    r3 = run_bass_kernel_spmd(_prog("l3", build_l3), l3_inputs(inp, h0T_list, oT_list), core_ids=cores).results
    out = np.empty((4, SEQ, D), np.float32)
    na = TOK - NMETA
    for b in range(4):
        out[b, :na] = r3[2 * b]["outT"].T[NMETA:TOK]
        out[b, na:] = r3[2 * b + 1]["outT"].T[128:128 + SEQ - na]
    return out


def kernel(**inputs):
    return kernel_fused(**inputs)
```

```python
from contextlib import ExitStack

import numpy as np
import ml_dtypes
import concourse.bass as bass
import concourse.mybir as mybir
from concourse.bass_utils import run_bass_kernel_spmd

F32 = mybir.dt.float32
BF16 = mybir.dt.bfloat16
AF = mybir.ActivationFunctionType
ALU = mybir.AluOpType

D = 1024
NC8 = 8
NMETA = 16
SEQ = 4096
LPAD = 4224
NBLK = 33
TOK = 2176
TILES = [(0, 512), (512, 512), (1024, 512), (1536, 512), (2048, 128)]
DFF = 2816
NJ = 22
GROUPS = [list(range(0, 6)), list(range(6, 12)), list(range(12, 17)), list(range(17, 22))]
EPS = 1e-6
POOLW = (2, 4, 8, 16)
NEG = -30000.0


class Prog:
    ENGS = ("pe", "act", "dve", "pool", "sp")

    def __init__(self, nc):
        self.nc = nc
        self.ops = []
        self.last_w = {}
        self.readers = {}

    def op(self, eng, fn, reads=(), writes=(), dma=None, inc=16):
        oid = len(self.ops)
        deps = set()
        for k in reads:
            w = self.last_w.get(k)
            if w is not None:
                deps.add(w)
        for k in writes:
            w = self.last_w.get(k)
            if w is not None:
                deps.add(w)
            for r in self.readers.get(k, ()):
                deps.add(r)
        deps.discard(oid)
        self.ops.append(dict(eng=eng, fn=fn, deps=deps, dma=dma, inc=inc))
        for k in reads:
            self.readers.setdefault(k, []).append(oid)
        for k in writes:
            self.last_w[k] = oid
            self.readers[k] = []
        return oid

    def build(self, final_keys=(), sem_es=None, prefix="", barrier=False):
        nc = self.nc
        ops = self.ops
        final_ops = set()
        for k in final_keys:
            if k in self.last_w:
                final_ops.add(self.last_w[k])
        needed = set(final_ops)
        if barrier:
            last_of = {}
            for i, o in enumerate(ops):
                if o["dma"] is None:
                    last_of[o["eng"]] = i
            needed.update(last_of.values())
        for o in ops:
            for d in o["deps"]:
                do = ops[d]
                if do["dma"] is None and o["dma"] is None and do["eng"] == o["eng"] == "pe":
                    continue
                needed.add(d)
        chan_names = [("eng", e) for e in self.ENGS]
        for o in ops:
            if o["dma"] is not None and ("dma", o["dma"]) not in chan_names:
                chan_names.append(("dma", o["dma"]))
        counters = {c: 0 for c in chan_names}
        for i, o in enumerate(ops):
            c = ("dma", o["dma"]) if o["dma"] is not None else ("eng", o["eng"])
            o["chan"] = c
            if o["dma"] is not None:
                counters[c] += o["inc"]
                o["tick"] = counters[c]
                o["sig"] = True
            elif i in needed:
                counters[c] += 1
                o["tick"] = counters[c]
                o["sig"] = True
            else:
                o["tick"] = None
                o["sig"] = False
        seen = {e: {} for e in self.ENGS}
        streams = {e: [] for e in self.ENGS}
        for i, o in enumerate(ops):
            e = o["eng"]
            for d in sorted(o["deps"]):
                do = ops[d]
                if not do["sig"]:
                    continue
                if do["dma"] is None and o["dma"] is None and do["eng"] == e == "pe":
                    continue
                c = do["chan"]
                if seen[e].get(c, 0) >= do["tick"]:
                    continue
                seen[e][c] = do["tick"]
                streams[e].append(("wait", c, do["tick"]))
            streams[e].append(("op", i))
        for f in sorted(final_ops):
            do = ops[f]
            c = do["chan"]
            if seen["sp"].get(c, 0) >= do["tick"]:
                continue
            seen["sp"][c] = do["tick"]
            streams["sp"].append(("wait", c, do["tick"]))
        if barrier:
            for e in self.ENGS:
                for c in chan_names:
                    if counters[c] > 0 and seen[e].get(c, 0) < counters[c]:
                        streams[e].append(("wait", c, counters[c]))
        self.n_sems = len(chan_names)
        with ExitStack() as es:
            sems = {}
            for c in chan_names:
                sems[c] = (sem_es or es).enter_context(nc.semaphore(prefix + "s_%s_%s" % c))
            block = es.enter_context(nc.Block())

            def run(engname, eng):
                for s in streams[engname]:
                    if s[0] == "wait":
                        eng.wait_ge(sems[s[1]], s[2])
                    else:
                        o = ops[s[1]]
                        ins = o["fn"](eng)
                        if o["sig"]:
                            ins.then_inc(sems[o["chan"]], o["inc"] if o["dma"] is not None else 1)

            @block.tensor
            def _(eng):
                run("pe", eng)

            @block.scalar
            def _(eng):
                run("act", eng)

            @block.vector
            def _(eng):
                run("dve", eng)

            @block.gpsimd
            def _(eng):
                run("pool", eng)

            @block.sync
            def _(eng):
                run("sp", eng)


class Ctx:
    pass


def alloc_common(nc, es, c):
    pfx = getattr(c, "pfx", "")
    sb = lambda name, shape, dt: es.enter_context(nc.sbuf_tensor(pfx + "sb_" + name, shape, dt))
    if not hasattr(c, "hT"):
        c.hT = sb("hT", [128, NC8, TOK], F32)
    c.nT = sb("nT", [128, NC8, TOK], BF16)
    c.aT = sb("aT", [128, 6, TOK], BF16)
    c.sq = sb("sq", [128, NC8, 512], BF16)
    c.rs = sb("rs", [128, 512], F32)
    c.rstd = sb("rstd", [128, 512], F32)
    c.ones = sb("ones", [128, 128], BF16)
    c.U = [sb("U%d" % i, [128, 2, 516], F32) for i in range(2)]
    c.C = [sb("C%d" % i, [128, 2, 512], F32) for i in range(2)]
    c.wup = [sb("wup%d" % i, [128, 2, NC8, 256], BF16) for i in range(2)]
    c.wd = [sb("wd%d" % i, [128, 6, 256], BF16) for i in range(2)]
    if not getattr(c, "no_wsq", False):
        c.wsq = [sb("wsq%d" % i, [128, NC8, 128], BF16) for i in range(2)]
    c.cw = sb("cw", [128, 44, 3], F32)
    c.cb = sb("cb", [128, 44], F32)
    c.gains = sb("gains", [128, 4, NC8], F32)
    if not hasattr(c, "ps"):
        c.ps = es.enter_context(nc.psum_tensor("ps", [128, 8, 512], F32))
    c.eps = sb("epsb", [128, 1], F32)


def emit_consts(P, c):
    P.op("pool", lambda e: e.memset(c.ones[:], 1.0), writes=["ones"])
    P.op("pool", lambda e: e.memset(c.eps[:], EPS), writes=["eps"])
    for i in range(2):
        P.op("pool", lambda e, i=i: e.memset(c.U[i][:, :, 0:2], 0.0), writes=[("U", i)])


def emit_rstd(P, c, ti):
    t0, n = TILES[ti]
    P.op("act", lambda e: e.activation(out=c.sq[:, :, 0:n], in_=c.hT[:, :, t0:t0 + n], func=AF.Square),
         reads=[("h", ti)], writes=["sq"])
    for k in range(NC8):
        P.op("pe", lambda e, k=k: e.matmul(c.ps[:, 0, 0:n], lhsT=c.ones[:], rhs=c.sq[:, k, 0:n],
                                          start=(k == 0), stop=(k == NC8 - 1)),
             reads=["sq", "ones"], writes=[("ps", 0)])
    P.op("act", lambda e: e.activation(out=c.rs[:, 0:n], in_=c.ps[:, 0, 0:n], func=AF.Sqrt,
                                       bias=c.eps[:], scale=1.0 / D),
         reads=[("ps", 0), "eps"], writes=["rs"])
    P.op("dve", lambda e: e.reciprocal(out=c.rstd[:, 0:n], in_=c.rs[:, 0:n]), reads=["rs"], writes=["rstd"])


def emit_norm_bf16(P, c, ti, gidx, dst, dkey):
    t0, n = TILES[ti]
    for k in range(NC8):
        P.op("dve", lambda e, k=k: e.scalar_tensor_tensor(
            out=dst[:, k, t0:t0 + n], in0=c.hT[:, k, t0:t0 + n], scalar=c.gains[:, gidx, k:k + 1],
            in1=c.rstd[:, 0:n], op0=ALU.mult, op1=ALU.mult),
            reads=[("h", ti), "rstd", "gains"], writes=[(dkey, ti)])


def emit_ffn(P, c, w_up, w_down):
    blocks = []
    for gi, grp in enumerate(GROUPS):
        for b0 in range(0, len(grp), 2):
            blocks.append((gi, grp[b0:b0 + 2], b0))

    def load_up(bi):
        _, js, _ = blocks[bi]
        wb = bi % 2
        nbk = len(js)
        for half, col in ((0, js[0] * 128), (1, DFF + js[0] * 128)):
            P.op("pool", lambda e, wb=wb, half=half, col=col, nbk=nbk: e.dma_start(
                out=c.wup[wb][:, half, :, 0:nbk * 128],
                in_=w_up[:, col:col + nbk * 128].rearrange("(k p) f -> p k f", p=128)),
                writes=[("wup", wb)], dma="wup%d" % wb)

    def load_down(gi, dq):
        grp = GROUPS[gi]
        ng = len(grp)
        r0 = grp[0] * 128
        db = dq % 2
        P.op("pool", lambda e: e.dma_start(
            out=c.wd[db][:, 0:ng, :],
            in_=w_down[r0:r0 + ng * 128, dq * 256:(dq + 1) * 256].rearrange("(j p) d -> p j d", p=128)),
            writes=[("wd", db)], dma="wd%d" % db)

    uctr = 0
    load_up(0)
    bi = 0
    for gi, grp in enumerate(GROUPS):
        gblocks = [blk for blk in blocks if blk[0] == gi]
        for bidx, (_, js, b0) in enumerate(gblocks):
            if bi + 1 < len(blocks):
                load_up(bi + 1)
            if bidx == 1:
                load_down(gi, 0)
                load_down(gi, 1)
            wb = bi % 2
            bi += 1
            for jo, j in enumerate(js):
                jj = b0 + jo
                for ti, (t0, n) in enumerate(TILES):
                    ub = uctr % 2
                    uctr += 1
                    for half in range(2):
                        bank = 1 + 2 * ub + half
                        for k in range(NC8):
                            P.op("pe", lambda e, half=half, k=k, bank=bank, wb=wb, t0=t0, n=n, jo=jo: e.matmul(
                                c.ps[:, bank, 0:n], lhsT=c.wup[wb][:, half, k, jo * 128:(jo + 1) * 128],
                                rhs=c.nT[:, k, t0:t0 + n], start=(k == 0), stop=(k == NC8 - 1)),
                                reads=[("wup", wb), ("n", ti)], writes=[("ps", bank)])
                    for half in range(2):
                        bank = 1 + 2 * ub + half
                        ch = j if half == 0 else NJ + j
                        P.op("act", lambda e, half=half, bank=bank, ub=ub, n=n: e.activation(
                            out=c.U[ub][:, half, 2:2 + n], in_=c.ps[:, bank, 0:n], func=AF.Identity),
                            reads=[("ps", bank)], writes=[("U", ub)])
                        P.op("act", lambda e, half=half, bank=bank, ub=ub, n=n, ch=ch: e.activation(
                            out=c.C[ub][:, half, 0:n], in_=c.ps[:, bank, 0:n], func=AF.Identity,
                            bias=c.cb[:, ch:ch + 1], scale=c.cw[:, ch, 2:3]),
                            reads=[("ps", bank), "cw"], writes=[("C", ub)])
                    for half in range(2):
                        ch = j if half == 0 else NJ + j
                        for sh in (1, 0):
                            P.op("dve", lambda e, half=half, ub=ub, n=n, ch=ch, sh=sh: e.scalar_tensor_tensor(
                                out=c.C[ub][:, half, 0:n], in0=c.U[ub][:, half, sh:sh + n],
                                scalar=c.cw[:, ch, sh:sh + 1], in1=c.C[ub][:, half, 0:n],
                                op0=ALU.mult, op1=ALU.add),
                                reads=[("U", ub), ("C", ub), "cw"], writes=[("C", ub)])
                    nb = (ub + 1) % 2
                    if ti + 1 < len(TILES):
                        P.op("pool", lambda e, ub=ub, nb=nb, n=n: e.tensor_copy(
                            out=c.U[nb][:, :, 0:2], in_=c.U[ub][:, :, n:n + 2]),
                            reads=[("U", ub)], writes=[("U", nb)])
                    else:
                        P.op("pool", lambda e, nb=nb: e.memset(c.U[nb][:, :, 0:2], 0.0), writes=[("U", nb)])
                    P.op("act", lambda e, ub=ub, n=n: e.activation(
                        out=c.U[ub][:, 0, 2:2 + n], in_=c.C[ub][:, 0, 0:n], func=AF.Silu),
                        reads=[("C", ub)], writes=[("U", ub)])
                    P.op("dve", lambda e, ub=ub, n=n, jj=jj, t0=t0: e.tensor_tensor(
                        out=c.aT[:, jj, t0:t0 + n], in0=c.U[ub][:, 0, 2:2 + n], in1=c.C[ub][:, 1, 0:n], op=ALU.mult),
                        reads=[("U", ub), ("C", ub)], writes=[("a", ti)])
        if len(gblocks) < 2:
            load_down(gi, 0)
            load_down(gi, 1)
        ng = len(grp)
        for dc in range(NC8):
            dq = dc // 2
            dh = dq % 2
            for ti, (t0, n) in enumerate(TILES):
                bank = 5 + (ti % 2)
                for jj in range(ng):
                    P.op("pe", lambda e, jj=jj, bank=bank, dh=dh, dc=dc, t0=t0, n=n, ng=ng: e.matmul(
                        c.ps[:, bank, 0:n], lhsT=c.wd[dh][:, jj, (dc % 2) * 128:(dc % 2 + 1) * 128],
                        rhs=c.aT[:, jj, t0:t0 + n], start=(jj == 0), stop=(jj == ng - 1)),
                        reads=[("wd", dh), ("a", ti)], writes=[("ps", bank)])
                P.op("dve", lambda e, bank=bank, dc=dc, t0=t0, n=n: e.tensor_tensor(
                    out=c.hT[:, dc, t0:t0 + n], in0=c.ps[:, bank, 0:n], in1=c.hT[:, dc, t0:t0 + n], op=ALU.add),
                    reads=[("ps", bank), ("h", ti)], writes=[("h", ti)])
            if dc % 2 == 1 and dq + 2 < 4:
                load_down(gi, dq + 2)


def emit_proj_fm(P, c, w, col0, nchunks, ti_list, consume, srcT=None, skey="n", tag="wsq"):
    srcT = c.nT if srcT is None else srcT
    for oc in range(nchunks):
        wb = c.wsq_ctr % 2
        c.wsq_ctr += 1
        col = col0 + oc * 128
        P.op("pool", lambda e, wb=wb, col=col: e.dma_start(
            out=c.wsq[wb][:], in_=w[:, col:col + 128].rearrange("(k p) f -> p k f", p=128)),
            writes=[("wsq", wb)], dma="wsq%d" % wb)
        for ti in ti_list:
            t0, n = TILES[ti]
            bank = 5 + (c.pbank_ctr % 2)
            c.pbank_ctr += 1
            for k in range(NC8):
                P.op("pe", lambda e, k=k, bank=bank, wb=wb, t0=t0, n=n: e.matmul(
                    c.ps[:, bank, 0:n], lhsT=c.wsq[wb][:, k, :], rhs=srcT[:, k, t0:t0 + n],
                    start=(k == 0), stop=(k == NC8 - 1)),
                    reads=[("wsq", wb), (skey, ti)], writes=[("ps", bank)])
            consume(oc, ti, bank)


def l1_front(nc, es, c, P, io):
    xT, gains_d, pscale_d, poolw_d, cwtab_d = io["xT"], io["gains"], io["pscale"], io["poolw"], io["cwtab"]
    w_up, cw_d, cb_d, w_down = io["w_up"], io["cw"], io["cb"], io["w_down"]
    pfx = getattr(c, "pfx", "")
    if True:
        sb = lambda name, shape, dtp: es.enter_context(nc.sbuf_tensor(pfx + "sb_" + name, shape, dtp))
        nbuf = sb("nbuf", [128, NC8, 16 + 512], F32)
        sA = sb("sA", [128, 16 + 512], F32)
        sB = sb("sB", [128, 16 + 512], F32)
        poolw = sb("poolw", [128, 4, 2, 256], BF16)
        pscale = sb("pscale", [128, NC8], F32)
        cwtab = sb("cwtab", [128, 4, 16], F32)
        tmp16 = sb("tmp16", [128, 16], F32)
        emit_consts(P, c)
        P.op("sp", lambda e: e.dma_start(out=c.gains[:], in_=gains_d), writes=["gains"], dma="c0")
        P.op("sp", lambda e: e.dma_start(out=pscale[:], in_=pscale_d), writes=["pscale"], dma="c1")
        P.op("sp", lambda e: e.dma_start(out=cwtab[:], in_=cwtab_d), writes=["cwtab"], dma="c2")
        P.op("sp", lambda e: e.dma_start(out=c.cw[:], in_=cw_d), writes=["cw"], dma="c3")
        P.op("sp", lambda e: e.dma_start(out=c.cb[:], in_=cb_d), writes=["cw"], dma="c3")
        P.op("pool", lambda e: e.dma_start(out=poolw[:], in_=poolw_d.rearrange("g (k p) d -> p g k d", p=128)),
             writes=["poolw"], dma="c4")
        for ti, (t0, n) in enumerate(TILES):
            P.op("sp", lambda e, t0=t0, n=n: e.dma_start(
                out=c.hT[:, :, t0:t0 + n], in_=xT[:, t0:t0 + n].rearrange("(k p) t -> p k t", p=128)),
                writes=[("h", ti)], dma="x%d" % ti)
        P.op("pool", lambda e: e.memset(nbuf[:, :, 0:16], 0.0), writes=["nbuf"])

        for ti, (t0, n) in enumerate(TILES):
            emit_rstd(P, c, ti)
            for k in range(NC8):
                P.op("dve", lambda e, k=k, t0=t0, n=n: e.scalar_tensor_tensor(
                    out=nbuf[:, k, 16:16 + n], in0=c.hT[:, k, t0:t0 + n], scalar=c.gains[:, 0, k:k + 1],
                    in1=c.rstd[:, 0:n], op0=ALU.mult, op1=ALU.mult),
                    reads=[("h", ti), "rstd", "gains"], writes=["nbuf"])
            for k in range(NC8):
                g = k // 2
                w = POOLW[g]
                src = nbuf[:, k, :]
                cur = None
                lvl = 1
                bufs = [sA, sB]
                bi = 0
                while lvl < w:
                    lo = 16 - (w - 2 * lvl) if (w - 2 * lvl) > 0 else 16
                    dst = bufs[bi]
                    prev = src if cur is None else cur
                    P.op("dve", lambda e, dst=dst, prev=prev, lo=lo, lvl=lvl, n=n: e.tensor_tensor(
                        out=dst[:, lo:16 + n], in0=prev[:, lo:16 + n], in1=prev[:, lo - lvl:16 + n - lvl], op=ALU.add),
                        reads=["nbuf", "sA", "sB"], writes=["sA" if bi == 0 else "sB"])
                    cur = dst
                    bi ^= 1
                    lvl *= 2
                P.op("dve", lambda e, cur=cur, k=k, w=w, t0=t0, n=n: e.scalar_tensor_tensor(
                    out=c.nT[:, k, t0:t0 + n], in0=cur[:, 16:16 + n], scalar=1.0 / w, in1=nbuf[:, k, 16:16 + n],
                    op0=ALU.mult, op1=ALU.subtract),
                    reads=["nbuf", "sA", "sB"], writes=[("n", ti)])
                if ti == 0:
                    P.op("dve", lambda e, cur=cur, g=g: e.tensor_tensor(
                        out=tmp16[:], in0=cur[:, 16:32], in1=cwtab[:, g, :], op=ALU.mult),
                        reads=["sA", "sB", "cwtab"], writes=["tmp16"])
                    P.op("dve", lambda e, k=k: e.tensor_tensor(
                        out=c.nT[:, k, 0:16], in0=tmp16[:], in1=nbuf[:, k, 16:32], op=ALU.subtract),
                        reads=["tmp16", "nbuf"], writes=[("n", ti)])
            if ti + 1 < len(TILES):
                P.op("pool", lambda e, n=n: e.tensor_copy(out=nbuf[:, :, 0:16], in_=nbuf[:, :, n:n + 16]),
                     reads=["nbuf"], writes=["nbuf"])
            for oc in range(NC8):
                g = oc // 2
                bank = 5 + (oc % 2)
                for kk in range(2):
                    P.op("pe", lambda e, g=g, kk=kk, oc=oc, bank=bank, t0=t0, n=n: e.matmul(
                        c.ps[:, bank, 0:n], lhsT=poolw[:, g, kk, (oc % 2) * 128:(oc % 2) * 128 + 128],
                        rhs=c.nT[:, 2 * g + kk, t0:t0 + n], start=(kk == 0), stop=(kk == 1)),
                        reads=["poolw", ("n", ti)], writes=[("ps", bank)])
                P.op("dve", lambda e, oc=oc, bank=bank, t0=t0, n=n: e.scalar_tensor_tensor(
                    out=c.hT[:, oc, t0:t0 + n], in0=c.ps[:, bank, 0:n], scalar=pscale[:, oc:oc + 1],
                    in1=c.hT[:, oc, t0:t0 + n], op0=ALU.mult, op1=ALU.add),
                    reads=[("ps", bank), ("h", ti), "pscale"], writes=[("h", ti)])
        for ti in range(len(TILES)):
            emit_rstd(P, c, ti)
            emit_norm_bf16(P, c, ti, 1, c.nT, "n")
        emit_ffn(P, c, w_up, w_down)


def build_l1():
    nc = bass.Bass("TRN2", target_bir_lowering=False)
    dt = lambda name, shape, dtype, kind: nc.dram_tensor(name, shape, dtype, kind=kind).ap()
    xT = dt("xT", [D, TOK], F32, "ExternalInput")
    gains_d = dt("gains", [128, 4, NC8], F32, "ExternalInput")
    pscale_d = dt("pscale", [128, NC8], F32, "ExternalInput")
    poolw_d = dt("poolw", [4, 256, 256], F32, "ExternalInput")
    cwtab_d = dt("cwtab", [128, 4, 16], F32, "ExternalInput")
    w_up = dt("w_up", [D, 2 * DFF], F32, "ExternalInput")
    cw_d = dt("cw", [128, 44, 3], F32, "ExternalInput")
    cb_d = dt("cb", [128, 44], F32, "ExternalInput")
    w_down = dt("w_down", [DFF, D], F32, "ExternalInput")
    w_q = dt("w_q", [D, D], F32, "ExternalInput")
    w_kv = dt("w_kv", [D, 2 * D], F32, "ExternalInput")
    h0T = dt("h0T", [D, TOK], F32, "ExternalOutput")
    qT = dt("qT", [D, TOK], BF16, "ExternalOutput")
    kT = dt("kT", [D, TOK], BF16, "ExternalOutput")
    vv = dt("v", [TOK, D], BF16, "ExternalOutput")

    with ExitStack() as es:
        c = Ctx()
        alloc_common(nc, es, c)
        c.wsq_ctr = 0
        c.pbank_ctr = 0
        P = Prog(nc)
        io = dict(xT=xT, gains=gains_d, pscale=pscale_d, poolw=poolw_d, cwtab=cwtab_d, w_up=w_up, cw=cw_d, cb=cb_d, w_down=w_down)
        l1_front(nc, es, c, P, io)
        sb = lambda name, shape, dtp: es.enter_context(nc.sbuf_tensor("sb_" + name, shape, dtp))
        wv = sb("wv", [128, NC8, 512], BF16)
        stg = [sb("stg%d" % i, [128, 512], BF16) for i in range(2)]
        vstg = stg
        for ti, (t0, n) in enumerate(TILES):
            P.op("sp", lambda e, t0=t0, n=n: e.dma_start(
                out=h0T[:, t0:t0 + n].rearrange("(k p) t -> p k t", p=128), in_=c.hT[:, :, t0:t0 + n]),
                reads=[("h", ti)], writes=["h0T"], dma="oh")
        sctr = [0]

        def evac_to(dst, scale):
            def consume(oc, ti, bank):
                t0, n = TILES[ti]
                s = sctr[0] % 2
                sctr[0] += 1
                P.op("act", lambda e: e.activation(out=stg[s][:, 0:n], in_=c.ps[:, bank, 0:n], func=AF.Identity, scale=scale),
                     reads=[("ps", bank)], writes=[("stg", s)])
                P.op("sp", lambda e: e.dma_start(out=dst[oc * 128:(oc + 1) * 128, t0:t0 + n], in_=stg[s][:, 0:n]),
                     reads=[("stg", s)], writes=["oq%d" % s], dma="stg%d" % s)
            return consume

        all_t = list(range(len(TILES)))
        for ti in all_t:
            emit_rstd(P, c, ti)
            emit_norm_bf16(P, c, ti, 3, c.nT, "n")
        emit_proj_fm(P, c, w_q, 0, NC8, all_t, evac_to(qT, 0.125))
        for ti in all_t:
            emit_rstd(P, c, ti)
            emit_norm_bf16(P, c, ti, 2, c.nT, "n")
        emit_proj_fm(P, c, w_kv, 0, NC8, all_t, evac_to(kT, 1.0))
        vctr = 0
        for hf in range(2):
            P.op("pool", lambda e, hf=hf: e.dma_start(
                out=wv[:], in_=w_kv[:, D + hf * 512:D + (hf + 1) * 512].rearrange("(k p) f -> p k f", p=128)),
                writes=["wv"], dma="wv")
            for ti, (t0, n) in enumerate(TILES):
                for b0 in range(0, n, 128):
                    s = vctr % 2
                    vctr += 1
                    bank = 5 + s
                    for k in range(NC8):
                        P.op("pe", lambda e, k=k, bank=bank, t0=t0, b0=b0: e.matmul(
                            c.ps[:, bank, :], lhsT=c.nT[:, k, t0 + b0:t0 + b0 + 128], rhs=wv[:, k, :],
                            start=(k == 0), stop=(k == NC8 - 1)),
                            reads=["wv", ("n", ti)], writes=[("ps", bank)])
                    P.op("act", lambda e, bank=bank, s=s: e.activation(
                        out=vstg[s][:], in_=c.ps[:, bank, :], func=AF.Identity),
                        reads=[("ps", bank)], writes=[("stg", s)])
                    P.op("sp", lambda e, s=s, t0=t0, b0=b0, hf=hf: e.dma_start(
                        out=vv[t0 + b0:t0 + b0 + 128, hf * 512:(hf + 1) * 512], in_=vstg[s][:]),
                        reads=[("stg", s)], writes=["oq%d" % s], dma="stg%d" % s)
        P.build(final_keys=["h0T", "oq0", "oq1"])
        c.P = P
    return nc


def l3_body(nc, es, c, P, io, load_o=None):
    h0T, oT, gains_d, w_o, w_up, cw_d, cb_d, w_down, outT = (io.get(k) for k in (
        "h0T", "oT", "gains", "w_o", "w_up", "cw", "cb", "w_down", "outT"))
    emit_consts(P, c)
    P.op("sp", lambda e: e.dma_start(out=c.gains[:], in_=gains_d), writes=["gains"], dma="c0")
    P.op("sp", lambda e: e.dma_start(out=c.cw[:], in_=cw_d), writes=["cw"], dma="c3")
    P.op("sp", lambda e: e.dma_start(out=c.cb[:], in_=cb_d), writes=["cw"], dma="c3")
    if load_o is None:
        for ti, (t0, n) in enumerate(TILES):
            P.op("sp", lambda e, t0=t0, n=n: e.dma_start(
                out=c.hT[:, :, t0:t0 + n], in_=h0T[:, t0:t0 + n].rearrange("(k p) t -> p k t", p=128)),
                writes=[("h", ti)], dma="x%d" % ti)
            P.op("sp", lambda e, t0=t0, n=n: e.dma_start(
                out=c.nT[:, :, t0:t0 + n], in_=oT[:, t0:t0 + n].rearrange("(k p) t -> p k t", p=128)),
                writes=[("n", ti)], dma="o%d" % ti)
    else:
        load_o()
    all_t = list(range(len(TILES)))

    def add_to_h(oc, ti, bank):
        t0, n = TILES[ti]
        P.op("dve", lambda e: e.tensor_tensor(
            out=c.hT[:, oc, t0:t0 + n], in0=c.ps[:, bank, 0:n], in1=c.hT[:, oc, t0:t0 + n], op=ALU.add),
            reads=[("ps", bank), ("h", ti)], writes=[("h", ti)])

    emit_proj_fm(P, c, w_o, 0, NC8, all_t, add_to_h)
    for ti in all_t:
        emit_rstd(P, c, ti)
        emit_norm_bf16(P, c, ti, 0, c.nT, "n")
    emit_ffn(P, c, w_up, w_down)
    octr = 0
    for ti, (t0, n) in enumerate(TILES):
        emit_rstd(P, c, ti)
        for k in range(NC8):
            s = octr % 4
            octr += 1
            stg = c.C[s // 2][:, s % 2, :]
            P.op("dve", lambda e, k=k, stg=stg, t0=t0, n=n: e.scalar_tensor_tensor(
                out=stg[:, 0:n], in0=c.hT[:, k, t0:t0 + n], scalar=c.gains[:, 1, k:k + 1],
                in1=c.rstd[:, 0:n], op0=ALU.mult, op1=ALU.mult),
                reads=[("h", ti), "rstd", "gains"], writes=[("ostg", s)])
            P.op("sp", lambda e, k=k, stg=stg, t0=t0, n=n: e.dma_start(
                out=outT[k * 128:(k + 1) * 128, t0:t0 + n], in_=stg[:, 0:n]),
                reads=[("ostg", s)], writes=["out%d" % s], dma="ostg%d" % s)


def build_l3():
    nc = bass.Bass("TRN2", target_bir_lowering=False)
    dt = lambda name, shape, dtype, kind: nc.dram_tensor(name, shape, dtype, kind=kind).ap()
    h0T = dt("h0T", [D, TOK], F32, "ExternalInput")
    oT = dt("oT", [D, TOK], BF16, "ExternalInput")
    gains_d = dt("gains", [128, 4, NC8], F32, "ExternalInput")
    w_o = dt("w_o", [D, D], F32, "ExternalInput")
    w_up = dt("w_up", [D, 2 * DFF], F32, "ExternalInput")
    cw_d = dt("cw", [128, 44, 3], F32, "ExternalInput")
    cb_d = dt("cb", [128, 44], F32, "ExternalInput")
    w_down = dt("w_down", [DFF, D], F32, "ExternalInput")
    outT = dt("outT", [D, TOK], F32, "ExternalOutput")
    with ExitStack() as es:
        c = Ctx()
        alloc_common(nc, es, c)
        c.wsq_ctr = 0
        c.pbank_ctr = 0
        P = Prog(nc)
        io = dict(h0T=h0T, oT=oT, gains=gains_d, w_o=w_o, w_up=w_up, cw=cw_d, cb=cb_d, w_down=w_down, outT=outT)
        l3_body(nc, es, c, P, io)
        P.build(final_keys=["out0", "out1", "out2", "out3"])
        c.P = P
    return nc


def l3_inputs(inp, h0T_list, oT_list):
    maps = []
    gains = np.zeros((128, 4, NC8), np.float32)
    gains[:, 0] = lay_gain(inp["ffn_norm"][1])
    gains[:, 1] = lay_gain(inp["final_norm"])
    cw = np.ascontiguousarray(inp["ffn_conv_w"][1].T.reshape(44, 128, 3).transpose(1, 0, 2))
    cb = np.ascontiguousarray(inp["ffn_conv_b"][1].reshape(44, 128).T)
    for ci in range(len(h0T_list)):
        maps.append({
            "h0T": h0T_list[ci], "oT": oT_list[ci], "gains": gains,
            "w_o": np.ascontiguousarray(inp["w_o"][0]),
            "w_up": np.ascontiguousarray(inp["ffn_w_up"][1]),
            "cw": cw, "cb": cb,
            "w_down": np.ascontiguousarray(inp["ffn_w_down"][1]),
        })
    return maps


QTILES = [(i * 512, 512) for i in range(8)] + [(4096, 128)]


def l2_body(nc, es, P, ps, qT, kT, vv, cst_d, nmask_d, emit_out, pfx=""):
    sb = lambda name, shape, dtp: es.enter_context(nc.sbuf_tensor(pfx + "sb_" + name, shape, dtp))
    qs = [sb("q%d" % i, [128, LPAD], BF16) for i in range(2)]
    ks = [sb("k%d" % i, [128, LPAD], BF16) for i in range(2)]
    vs = [sb("v%d" % i, [128, NBLK, 128], BF16) for i in range(2)]
    osb = sb("osb", [128, 4, LPAD], BF16)
    E = [sb("E%d" % i, [128, 2, 512], F32) for i in range(2)]
    SP = [sb("SP%d" % i, [128, 2, 512], BF16) for i in range(2)]
    A = [sb("A%d" % i, [128, 2, 512], BF16) for i in range(2)]
    Rb = [sb("R%d" % i, [128, 2, 512], BF16) for i in range(2)]
    cst = sb("cst", [128, 3, 128], BF16)
    nmask = sb("nmask", [128, 4, 512], BF16)
    onec = sb("onec", [128, 1], F32)
    if ps is None:
        ps = es.enter_context(nc.psum_tensor("ps", [128, 8, 512], F32))
    P.op("pool", lambda e: e.memset(onec[:], 1.0), writes=["onec"])
    P.op("sp", lambda e: e.dma_start(out=cst[:], in_=cst_d), writes=["cst"], dma="c0")
    P.op("sp", lambda e: e.dma_start(out=nmask[:], in_=nmask_d), writes=["nmask"], dma="c1")
    units = []
    for hp in range(4):
        for qi, (q0, nq) in enumerate(QTILES):
            kmax = (q0 + nq) // 128 - 1
            kdiag0 = q0 // 128
            for i, kb in enumerate(range(kmax, -1, -1)):
                units.append(dict(hp=hp, qi=qi, q0=q0, nq=nq, kb=kb, i=i, last=(kb == 0),
                                  r=(kb - kdiag0) if kb >= kdiag0 else None))
    loaded = set()

    def ensure_loaded(hp):
        if hp in loaded or hp >= 4:
            return
        loaded.add(hp)
        b = hp % 2
        P.op("sp", lambda e: e.dma_start(out=qs[b][:], in_=qT[hp * 128:(hp + 1) * 128, :]),
             reads=["qT_d0", "qT_d1"], writes=[("q", b)], dma="q%d" % b)
        P.op("sp", lambda e: e.dma_start(out=ks[b][:], in_=kT[hp * 128:(hp + 1) * 128, :]),
             reads=["kT_d0", "kT_d1"], writes=[("k", b)], dma="k%d" % b)
        P.op("sp", lambda e: e.dma_start(
            out=vs[b][:], in_=vv[:, hp * 128:(hp + 1) * 128].rearrange("(k p) f -> p k f", p=128)),
            reads=["v_d0", "v_d1"], writes=[("v", b)], dma="v%d" % b)

    def s1a(u, idx):
        b = u["hp"] % 2
        zb = 2 * (idx % 2)
        q0, nq, kb = u["q0"], u["nq"], u["kb"]
        for h in range(2):
            P.op("pe", lambda e, h=h: e.matmul(
                ps[:, zb + h, 0:nq], lhsT=ks[b][64 * h:64 * h + 64, kb * 128:(kb + 1) * 128],
                rhs=qs[b][64 * h:64 * h + 64, q0:q0 + nq], start=True, stop=(u["r"] is None)),
                reads=[("q", b), ("k", b)], writes=[("z", idx % 2)])
            if u["r"] is not None:
                P.op("pe", lambda e, h=h: e.matmul(
                    ps[:, zb + h, 0:nq], lhsT=cst[:, 2, :], rhs=nmask[:, u["r"], 0:nq], start=False, stop=True),
                    reads=["cst", "nmask"], writes=[("z", idx % 2)])
        P.op("act", lambda e: e.activation(out=E[idx % 2][:, :, 0:nq], in_=ps[:, zb:zb + 2, 0:nq], func=AF.Exp),
             reads=[("z", idx % 2)], writes=[("E", idx % 2)])

    def s1b(u, idx):
        nq = u["nq"]
        P.op("act", lambda e: e.activation(out=SP[idx % 2][:, :, 0:nq], in_=E[idx % 2][:, :, 0:nq],
                                           func=AF.Ln, bias=onec[:]),
             reads=[("E", idx % 2), "onec"], writes=[("SP", idx % 2)])
        if not u["last"]:
            i = u["i"]
            if i == 0:
                P.op("pool", lambda e: e.tensor_copy(out=Rb[1][:, :, 0:nq], in_=SP[idx % 2][:, :, 0:nq]),
                     reads=[("SP", idx % 2)], writes=[("R", 1)])
            else:
                P.op("pool", lambda e: e.tensor_tensor(
                    out=Rb[(i + 1) % 2][:, :, 0:nq], in0=Rb[i % 2][:, :, 0:nq], in1=SP[idx % 2][:, :, 0:nq], op=ALU.add),
                    reads=[("SP", idx % 2), ("R", i % 2)], writes=[("R", (i + 1) % 2)])

    def s2(u, idx):
        zb = 2 * (idx % 2)
        nq, i = u["nq"], u["i"]
        for h in range(2):
            P.op("pe", lambda e, h=h: e.matmul(
                ps[:, zb + h, 0:nq], lhsT=cst[:, 0, :], rhs=SP[idx % 2][:, h, 0:nq], start=False, stop=(i == 0)),
                reads=["cst", ("SP", idx % 2), ("E", idx % 2)], writes=[("z", idx % 2)])
            if i > 0:
                P.op("pe", lambda e, h=h: e.matmul(
                    ps[:, zb + h, 0:nq], lhsT=cst[:, 1, :], rhs=Rb[i % 2][:, h, 0:nq], start=False, stop=True),
                    reads=["cst", ("R", i % 2)], writes=[("z", idx % 2)])
        P.op("act", lambda e: e.activation(out=A[idx % 2][:, :, 0:nq], in_=ps[:, zb:zb + 2, 0:nq], func=AF.Exp),
             reads=[("z", idx % 2)], writes=[("A", idx % 2)])

    def s3(u, idx):
        b = u["hp"] % 2
        ob = 4 + (u["qi"] % 2)
        q0, nq, kb, i, hp = u["q0"], u["nq"], u["kb"], u["i"], u["hp"]
        for h in range(2):
            P.op("pe", lambda e, h=h: e.matmul(
                ps[64 * h:64 * h + 64, ob, 0:nq], lhsT=vs[b][:, kb, 64 * h:64 * h + 64],
                rhs=A[idx % 2][:, h, 0:nq], start=(i == 0), stop=u["last"]),
                reads=[("v", b), ("A", idx % 2)], writes=[("o", ob)])
        if u["last"]:
            P.op("dve", lambda e: e.tensor_copy(out=osb[:, hp, q0:q0 + nq], in_=ps[:, ob, 0:nq]),
                 reads=[("o", ob)], writes=[("osb", hp)])
            if u["qi"] == len(QTILES) - 1:
                emit_out(hp, osb)

    n = len(units)
    ensure_loaded(0)
    for idx in range(n + 2):
        if idx < n:
            u = units[idx]
            ensure_loaded(u["hp"])
            if u["qi"] == 4 and u["i"] == 0:
                ensure_loaded(u["hp"] + 1)
            s1a(u, idx)
        if 0 <= idx - 1 < n:
            s2(units[idx - 1], idx - 1)
        if idx < n:
            s1b(units[idx], idx)
        if 0 <= idx - 2 < n:
            s3(units[idx - 2], idx - 2)


def build_l2():
    nc = bass.Bass("TRN2", target_bir_lowering=False)
    dt = lambda name, shape, dtype, kind: nc.dram_tensor(name, shape, dtype, kind=kind).ap()
    qT = dt("qT", [512, LPAD], BF16, "ExternalInput")
    kT = dt("kT", [512, LPAD], BF16, "ExternalInput")
    vv = dt("v", [LPAD, 512], BF16, "ExternalInput")
    cst_d = dt("cst", [128, 3, 128], BF16, "ExternalInput")
    nmask_d = dt("nmask", [128, 4, 512], BF16, "ExternalInput")
    oT = dt("oT", [512, LPAD], BF16, "ExternalOutput")
    with ExitStack() as es:
        P = Prog(nc)

        def emit_out(hp, osb):
            P.op("sp", lambda e: e.dma_start(out=oT[hp * 128:(hp + 1) * 128, :], in_=osb[:, hp, :]),
                 reads=[("osb", hp)], writes=["oT%d" % hp], dma="oT%d" % hp)

        l2_body(nc, es, P, None, qT, kT, vv, cst_d, nmask_d, emit_out)
        P.build(final_keys=["oT0", "oT1", "oT2", "oT3"])
    return nc


def l2_consts():
    j = np.arange(128)
    ntri = np.where(j[:, None] >= j[None, :], -1.0, 0.0)
    cst = np.stack([ntri, -np.ones((128, 128)), np.eye(128)], axis=1).astype(ml_dtypes.bfloat16)
    t = np.arange(512)
    nmask = np.zeros((128, 4, 512), np.float32)
    for r in range(4):
        nmask[:, r, :] = np.where((128 * r + j)[:, None] >= t[None, :], NEG, 0.0)
    return np.ascontiguousarray(cst), nmask.astype(ml_dtypes.bfloat16)


RG = [[0, 1], [2, 3], [4, 5], [6, 7]]


def build_fused():
    nc = bass.Bass("TRN2", target_bir_lowering=False)
    dt = lambda name, shape, dtype, kind="ExternalInput": nc.dram_tensor(name, shape, dtype, kind=kind).ap()
    io1 = dict(xT=dt("xT", [D, TOK], F32), gains=dt("gains1", [128, 4, NC8], F32), pscale=dt("pscale", [128, NC8], F32),
               poolw=dt("poolw", [4, 256, 256], F32), cwtab=dt("cwtab", [128, 4, 16], F32),
               w_up=dt("w_up0", [D, 2 * DFF], F32), cw=dt("cw0", [128, 44, 3], F32), cb=dt("cb0", [128, 44], F32),
               w_down=dt("w_down0", [DFF, D], F32))
    wqkv_d = [dt("wq_hg", [D, 512], F32), dt("wk_hg", [D, 512], F32), dt("wv_hg", [D, 512], F32)]
    cst_d = dt("cst", [128, 3, 128], BF16)
    nmask_d = dt("nmask", [128, 4, 512], BF16)
    sel_d = dt("sel", [128, 2], F32)
    io3 = dict(gains=dt("gains3", [128, 4, NC8], F32), w_o=dt("w_o", [D, D], F32), w_up=dt("w_up1", [D, 2 * DFF], F32),
               cw=dt("cw1", [128, 44, 3], F32), cb=dt("cb1", [128, 44], F32), w_down=dt("w_down1", [DFF, D], F32),
               outT=dt("outT", [D, TOK], F32, "ExternalOutput"))
    snd = [dt("snd%d" % i, [256, TOK], BF16, "Internal") for i in range(2)]
    rcv = [dt("rcv%d" % i, [512, TOK], BF16, "Internal") for i in range(2)]
    nall = [dt("nall%d" % i, [D, LPAD], BF16, "Internal") for i in range(2)]
    qT_d = dt("qT_d", [512, LPAD], BF16, "Internal")
    kT_d = dt("kT_d", [512, LPAD], BF16, "Internal")
    v_d = dt("v_d", [LPAD, 512], BF16, "Internal")
    osnd = [dt("osnd%d" % i, [128, LPAD], BF16, "Internal") for i in range(2)]
    orcv = [dt("orcv%d" % i, [256, LPAD], BF16, "Internal") for i in range(2)]
    oall = dt("oall", [D, LPAD], BF16, "Internal")
    all_t = list(range(len(TILES)))

    def allgather(P, src, dst, skey, dkey):
        P.op("pool", lambda e: e.collective_compute("AllGather", ALU.bypass, replica_groups=RG,
                                                    ins=[src.opt()], outs=[dst.opt()]),
             reads=[skey], writes=[dkey], dma="cc", inc=1)

    with ExitStack() as outer:
        hT = outer.enter_context(nc.sbuf_tensor("sb_hT", [128, NC8, TOK], F32))
        ps = outer.enter_context(nc.psum_tensor("ps", [128, 8, 512], F32))
        with ExitStack() as es:
            c = Ctx()
            c.pfx, c.hT, c.ps = "p1", hT, ps
            c.no_wsq = True
            alloc_common(nc, es, c)
            c.wsq_ctr = 0
            c.pbank_ctr = 0
            P = Prog(nc)
            l1_front(nc, es, c, P, io1)
            cc = 0
            for which, gidx in ((0, 3), (1, 2)):
                for ti in all_t:
                    emit_rstd(P, c, ti)
                    emit_norm_bf16(P, c, ti, gidx, c.nT, "n")
                for j in range(4):
                    b = cc % 2
                    cc += 1
                    for ti, (t0, n) in enumerate(TILES):
                        P.op("sp", lambda e, b=b, j=j, t0=t0, n=n: e.dma_start(
                            out=snd[b].rearrange("(k p) t -> p k t", p=128)[:, :, t0:t0 + n],
                            in_=c.nT[:, 2 * j:2 * j + 2, t0:t0 + n]),
                            reads=[("n", ti)], writes=[("snd", b)], dma="snd%d" % b)
                    allgather(P, snd[b], rcv[b], ("snd", b), ("rcv", b))
                    P.op("sp", lambda e, b=b, j=j, which=which: e.dma_start(
                        out=nall[which][256 * j:256 * j + 256, 0:TOK], in_=rcv[b][0:256, :]),
                        reads=[("rcv", b)], writes=["nall_a%d" % b], dma="na%d" % b)
                    P.op("sp", lambda e, b=b, j=j, which=which: e.dma_start(
                        out=nall[which][256 * j:256 * j + 256, TOK:LPAD], in_=rcv[b][256:512, 128:TOK]),
                        reads=[("rcv", b)], writes=["nall_b%d" % b], dma="nb%d" % b)
            P.build(final_keys=[], sem_es=outer, prefix="p1", barrier=True)
        with ExitStack() as es:
            sb = lambda name, shape, dtp: es.enter_context(nc.sbuf_tensor("p2a_" + name, shape, dtp))
            wsb = [sb("w%d" % i, [128, NC8, 512], BF16) for i in range(3)]
            ntile = [sb("nt%d" % i, [128, 2, NC8, 512], BF16) for i in range(2)]
            stg = [sb("stg%d" % i, [128, 512], BF16) for i in range(2)]
            P = Prog(nc)
            for i in range(3):
                P.op("pool", lambda e, i=i: e.dma_start(out=wsb[i][:], in_=wqkv_d[i].rearrange("(k p) f -> p k f", p=128)),
                     writes=[("w", i)], dma="w%d" % i)
            sctr = 0
            bctr = 0
            for qi, (t0, n) in enumerate(QTILES):
                nb = qi % 2
                for which in range(2):
                    P.op("sp", lambda e, nb=nb, which=which, t0=t0, n=n: e.dma_start(
                        out=ntile[nb][:, which, :, 0:n],
                        in_=nall[which][:, t0:t0 + n].rearrange("(k p) t -> p k t", p=128)),
                        writes=[("nt", nb)], dma="nt%d" % nb)
                for which, dst, scale, key in ((0, qT_d, 0.125, "qT_d"), (1, kT_d, 1.0, "kT_d")):
                    for oc in range(4):
                        bank = 6 + (bctr % 2)
                        bctr += 1
                        for k in range(NC8):
                            P.op("pe", lambda e, which=which, oc=oc, k=k, bank=bank, nb=nb, n=n: e.matmul(
                                ps[:, bank, 0:n], lhsT=wsb[which][:, k, oc * 128:(oc + 1) * 128],
                                rhs=ntile[nb][:, which, k, 0:n], start=(k == 0), stop=(k == NC8 - 1)),
                                reads=[("w", which), ("nt", nb)], writes=[("ps", bank)])
                        s_ = sctr % 2
                        sctr += 1
                        P.op("act", lambda e, s_=s_, bank=bank, n=n, scale=scale: e.activation(
                            out=stg[s_][:, 0:n], in_=ps[:, bank, 0:n], func=AF.Identity, scale=scale),
                            reads=[("ps", bank)], writes=[("stg", s_)])
                        P.op("sp", lambda e, s_=s_, dst=dst, oc=oc, t0=t0, n=n: e.dma_start(
                            out=dst[oc * 128:(oc + 1) * 128, t0:t0 + n], in_=stg[s_][:, 0:n]),
                            reads=[("stg", s_)], writes=["%s%d" % (key, s_)], dma="stg%d" % s_)
                for b0 in range(0, n, 128):
                    bank = 6 + (bctr % 2)
                    bctr += 1
                    for k in range(NC8):
                        P.op("pe", lambda e, k=k, bank=bank, nb=nb, b0=b0: e.matmul(
                            ps[:, bank, :], lhsT=ntile[nb][:, 1, k, b0:b0 + 128], rhs=wsb[2][:, k, :],
                            start=(k == 0), stop=(k == NC8 - 1)),
                            reads=[("w", 2), ("nt", nb)], writes=[("ps", bank)])
                    s_ = sctr % 2
                    sctr += 1
                    P.op("act", lambda e, s_=s_, bank=bank: e.activation(
                        out=stg[s_][:], in_=ps[:, bank, :], func=AF.Identity),
                        reads=[("ps", bank)], writes=[("stg", s_)])
                    P.op("sp", lambda e, s_=s_, t0=t0, b0=b0: e.dma_start(
                        out=v_d[t0 + b0:t0 + b0 + 128, :], in_=stg[s_][:]),
                        reads=[("stg", s_)], writes=["v_d%d" % s_], dma="stg%d" % s_)
            P.build(final_keys=[], sem_es=outer, prefix="p2a", barrier=True)
        with ExitStack() as es:
            P = Prog(nc)

            def emit_out(hp, osb):
                b = hp % 2
                P.op("sp", lambda e: e.dma_start(out=osnd[b], in_=osb[:, hp, :]),
                     reads=[("osb", hp)], writes=[("osnd", b)], dma="osnd%d" % b)
                allgather(P, osnd[b], orcv[b], ("osnd", b), ("orcv", b))
                for s_ in range(2):
                    P.op("sp", lambda e, s_=s_: e.dma_start(
                        out=oall[s_ * 512 + hp * 128:s_ * 512 + (hp + 1) * 128, :], in_=orcv[b][s_ * 128:(s_ + 1) * 128, :]),
                        reads=[("orcv", b)], writes=["oall%d_%d" % (b, s_)], dma="oa%d_%d" % (b, s_))

            l2_body(nc, es, P, ps, qT_d, kT_d, v_d, cst_d, nmask_d, emit_out, pfx="p2b")
            P.build(final_keys=[], sem_es=outer, prefix="p2b", barrier=True)
        with ExitStack() as es:
            c = Ctx()
            c.pfx, c.hT, c.ps = "p3", hT, ps
            alloc_common(nc, es, c)
            c.wsq_ctr = 0
            c.pbank_ctr = 0
            sel = es.enter_context(nc.sbuf_tensor("p3_sel", [128, 2], F32))
            P = Prog(nc)

            def load_o():
                P.op("sp", lambda e: e.dma_start(out=sel[:], in_=sel_d), writes=["sel"], dma="sel")
                for ti, (t0, n) in enumerate(TILES):
                    P.op("sp", lambda e, t0=t0, n=n: e.dma_start(
                        out=c.nT[:, :, t0:t0 + n], in_=oall[:, t0:t0 + n].rearrange("(k p) t -> p k t", p=128)),
                        writes=[("n", ti)], dma="o%d" % ti)
                    P.op("sp", lambda e, t0=t0, n=n: e.dma_start(
                        out=c.sq[:, :, 0:n],
                        in_=oall[:, LPAD - TOK + t0:LPAD - TOK + t0 + n].rearrange("(k p) t -> p k t", p=128)),
                        writes=["sq"], dma="osq")
                    P.op("dve", lambda e, t0=t0, n=n: e.tensor_scalar(
                        out=c.nT[:, :, t0:t0 + n], in0=c.nT[:, :, t0:t0 + n], scalar1=sel[:, 0:1], scalar2=None, op0=ALU.mult),
                        reads=[("n", ti), "sel"], writes=[("n", ti)])
                    P.op("dve", lambda e, t0=t0, n=n: e.scalar_tensor_tensor(
                        out=c.nT[:, :, t0:t0 + n], in0=c.sq[:, :, 0:n], scalar=sel[:, 1:2], in1=c.nT[:, :, t0:t0 + n],
                        op0=ALU.mult, op1=ALU.add),
                        reads=["sq", ("n", ti), "sel"], writes=[("n", ti)])

            l3_body(nc, es, c, P, io3, load_o)
            P.build(final_keys=["out0", "out1", "out2", "out3"], sem_es=outer, prefix="p3")
    return nc


def fused_inputs(inp):
    maps = []
    gains1 = np.stack([lay_gain(inp["mix_norm"][0]), lay_gain(inp["ffn_norm"][0]),
                       lay_gain(inp["kv_norm"]), lay_gain(inp["mix_norm"][1])], axis=1)
    gains3 = np.zeros((128, 4, NC8), np.float32)
    gains3[:, 0] = lay_gain(inp["ffn_norm"][1])
    gains3[:, 1] = lay_gain(inp["final_norm"])
    lay_cw = lambda w: np.ascontiguousarray(w.T.reshape(44, 128, 3).transpose(1, 0, 2))
    lay_cb = lambda b: np.ascontiguousarray(b.reshape(44, 128).T)
    cst, nmask = l2_consts()
    ca = np.ascontiguousarray
    for b in range(4):
        hs = pad_seq(inp["x"][b], inp["meta_tokens"])
        for half in range(2):
            t0 = core_tok0(half)
            cwt = np.zeros((128, 4, 16), np.float32)
            for g, w in enumerate(POOLW):
                for t in range(16):
                    cwt[:, g, t] = 1.0 / (min(w, t + 1) if half == 0 else w)
            sel = np.zeros((128, 2), np.float32)
            sel[:, half] = 1.0
            hg = slice(half * 512, (half + 1) * 512)
            maps.append({
                "xT": ca(hs[t0:t0 + TOK].T), "gains1": ca(gains1), "pscale": lay_gain(inp["pool_scale"][0]),
                "poolw": ca(inp["pool_w"][0]), "cwtab": cwt,
                "w_up0": ca(inp["ffn_w_up"][0]), "cw0": lay_cw(inp["ffn_conv_w"][0]), "cb0": lay_cb(inp["ffn_conv_b"][0]),
                "w_down0": ca(inp["ffn_w_down"][0]),
                "wq_hg": ca(inp["w_q"][0][:, hg]), "wk_hg": ca(inp["w_kv"][:, hg]),
                "wv_hg": ca(inp["w_kv"][:, D + half * 512:D + (half + 1) * 512]),
                "cst": cst, "nmask": nmask, "sel": sel,
                "gains3": gains3, "w_o": ca(inp["w_o"][0]),
                "w_up1": ca(inp["ffn_w_up"][1]), "cw1": lay_cw(inp["ffn_conv_w"][1]), "cb1": lay_cb(inp["ffn_conv_b"][1]),
                "w_down1": ca(inp["ffn_w_down"][1]),
            })
    return maps


def kernel_fused(**inputs):
    inp = {k: np.asarray(v) for k, v in inputs.items()}
    r = run_bass_kernel_spmd(_prog("fused", build_fused), fused_inputs(inp), core_ids=list(range(8))).results
    out = np.empty((4, SEQ, D), np.float32)
    na = TOK - NMETA
    for b in range(4):
        out[b, :na] = r[2 * b]["outT"].T[NMETA:TOK]
        out[b, na:] = r[2 * b + 1]["outT"].T[128:128 + SEQ - na]
    return out


def lay_gain(g):
    return np.ascontiguousarray(g.reshape(NC8, 128).T)


def pad_seq(x_b, meta):
    out = np.zeros((LPAD, D), np.float32)
    out[:NMETA] = meta
    out[NMETA:NMETA + SEQ] = x_b
    return out


def core_tok0(half):
    return 0 if half == 0 else LPAD - TOK


def l1_inputs(inp):
    maps = []
    gains = np.stack([lay_gain(inp["mix_norm"][0]), lay_gain(inp["ffn_norm"][0]),
                      lay_gain(inp["kv_norm"]), lay_gain(inp["mix_norm"][1])], axis=1)
    cw = np.ascontiguousarray(inp["ffn_conv_w"][0].T.reshape(44, 128, 3).transpose(1, 0, 2))
    cb = np.ascontiguousarray(inp["ffn_conv_b"][0].reshape(44, 128).T)
    for b in range(4):
        hs = pad_seq(inp["x"][b], inp["meta_tokens"])
        for half in range(2):
            t0 = core_tok0(half)
            cwt = np.zeros((128, 4, 16), np.float32)
            for g, w in enumerate(POOLW):
                for t in range(16):
                    cwt[:, g, t] = 1.0 / (min(w, t + 1) if half == 0 else w)
            maps.append({
                "xT": np.ascontiguousarray(hs[t0:t0 + TOK].T),
                "gains": np.ascontiguousarray(gains),
                "pscale": lay_gain(inp["pool_scale"][0]),
                "poolw": np.ascontiguousarray(inp["pool_w"][0]),
                "cwtab": cwt,
                "w_up": np.ascontiguousarray(inp["ffn_w_up"][0]),
                "cw": cw, "cb": cb,
                "w_down": np.ascontiguousarray(inp["ffn_w_down"][0]),
                "w_q": np.ascontiguousarray(inp["w_q"][0]),
                "w_kv": np.ascontiguousarray(inp["w_kv"]),
            })
    return maps


_PROGS = {}


def _prog(name, builder):
    if name not in _PROGS:
        _PROGS[name] = builder()
    return _PROGS[name]


def kernel_unfused(**inputs):
    inp = {k: np.asarray(v) for k, v in inputs.items()}
    cores = list(range(8))
    bf = ml_dtypes.bfloat16
    r1 = run_bass_kernel_spmd(_prog("l1", build_l1), l1_inputs(inp), core_ids=cores).results
    cst, nmask = l2_consts()
    maps2 = []
    for b in range(4):
        a, bb = r1[2 * b], r1[2 * b + 1]
        qf = np.concatenate([a["qT"], bb["qT"][:, 128:]], axis=1)
        kf = np.concatenate([a["kT"], bb["kT"][:, 128:]], axis=1)
        vf = np.concatenate([a["v"], bb["v"][128:]], axis=0)
        for hg in range(2):
            sl = slice(hg * 512, (hg + 1) * 512)
            maps2.append({"qT": np.ascontiguousarray(qf[sl]), "kT": np.ascontiguousarray(kf[sl]),
                          "v": np.ascontiguousarray(vf[:, sl]), "cst": cst, "nmask": nmask})
    r2 = run_bass_kernel_spmd(_prog("l2", build_l2), maps2, core_ids=cores).results
    h0T_list, oT_list = [], []
    for b in range(4):
        of = np.concatenate([r2[2 * b]["oT"], r2[2 * b + 1]["oT"]], axis=0)
        for half in range(2):
            t0 = core_tok0(half)
            h0T_list.append(r1[2 * b + half]["h0T"])
            oT_list.append(np.ascontiguousarray(of[:, t0:t0 + TOK]))
    r3 = run_bass_kernel_spmd(_prog("l3", build_l3), l3_inputs(inp, h0T_list, oT_list), core_ids=cores).results
    out = np.empty((4, SEQ, D), np.float32)
    na = TOK - NMETA
    for b in range(4):
        out[b, :na] = r3[2 * b]["outT"].T[NMETA:TOK]
        out[b, na:] = r3[2 * b + 1]["outT"].T[128:128 + SEQ - na]
    return out


def kernel(**inputs):
    return kernel_fused(**inputs)
```

```python
from contextlib import ExitStack

import numpy as np
import ml_dtypes
import concourse.bass as bass
import concourse.mybir as mybir
from concourse.bass_utils import run_bass_kernel_spmd

F32 = mybir.dt.float32
BF16 = mybir.dt.bfloat16
AF = mybir.ActivationFunctionType
ALU = mybir.AluOpType

D = 1024
NC8 = 8
NMETA = 16
SEQ = 4096
LPAD = 4224
NBLK = 33
TOK = 2176
TILES = [(0, 512), (512, 512), (1024, 512), (1536, 512), (2048, 128)]
DFF = 2816
NJ = 22
GROUPS = [list(range(0, 6)), list(range(6, 12)), list(range(12, 17)), list(range(17, 22))]
EPS = 1e-6
POOLW = (2, 4, 8, 16)
NEG = -30000.0


class Prog:
    ENGS = ("pe", "act", "dve", "pool", "sp")

    def __init__(self, nc):
        self.nc = nc
        self.ops = []
        self.last_w = {}
        self.readers = {}

    def op(self, eng, fn, reads=(), writes=(), dma=None, inc=16):
        oid = len(self.ops)
        deps = set()
        for k in reads:
            w = self.last_w.get(k)
            if w is not None:
                deps.add(w)
        for k in writes:
            w = self.last_w.get(k)
            if w is not None:
                deps.add(w)
            for r in self.readers.get(k, ()):
                deps.add(r)
        deps.discard(oid)
        self.ops.append(dict(eng=eng, fn=fn, deps=deps, dma=dma, inc=inc))
        for k in reads:
            self.readers.setdefault(k, []).append(oid)
        for k in writes:
            self.last_w[k] = oid
            self.readers[k] = []
        return oid

    def build(self, final_keys=(), sem_es=None, prefix="", barrier=False):
        nc = self.nc
        ops = self.ops
        final_ops = set()
        for k in final_keys:
            if k in self.last_w:
                final_ops.add(self.last_w[k])
        needed = set(final_ops)
        if barrier:
            last_of = {}
            for i, o in enumerate(ops):
                if o["dma"] is None:
                    last_of[o["eng"]] = i
            needed.update(last_of.values())
        for o in ops:
            for d in o["deps"]:
                do = ops[d]
                if do["dma"] is None and o["dma"] is None and do["eng"] == o["eng"] == "pe":
                    continue
                needed.add(d)
        chan_names = [("eng", e) for e in self.ENGS]
        for o in ops:
            if o["dma"] is not None and ("dma", o["dma"]) not in chan_names:
                chan_names.append(("dma", o["dma"]))
        counters = {c: 0 for c in chan_names}
        for i, o in enumerate(ops):
            c = ("dma", o["dma"]) if o["dma"] is not None else ("eng", o["eng"])
            o["chan"] = c
            if o["dma"] is not None:
                counters[c] += o["inc"]
                o["tick"] = counters[c]
                o["sig"] = True
            elif i in needed:
                counters[c] += 1
                o["tick"] = counters[c]
                o["sig"] = True
            else:
                o["tick"] = None
                o["sig"] = False
        seen = {e: {} for e in self.ENGS}
        streams = {e: [] for e in self.ENGS}
        for i, o in enumerate(ops):
            e = o["eng"]
            for d in sorted(o["deps"]):
                do = ops[d]
                if not do["sig"]:
                    continue
                if do["dma"] is None and o["dma"] is None and do["eng"] == e == "pe":
                    continue
                c = do["chan"]
                if seen[e].get(c, 0) >= do["tick"]:
                    continue
                seen[e][c] = do["tick"]
                streams[e].append(("wait", c, do["tick"]))
            streams[e].append(("op", i))
        for f in sorted(final_ops):
            do = ops[f]
            c = do["chan"]
            if seen["sp"].get(c, 0) >= do["tick"]:
                continue
            seen["sp"][c] = do["tick"]
            streams["sp"].append(("wait", c, do["tick"]))
        if barrier:
            for e in self.ENGS:
                for c in chan_names:
                    if counters[c] > 0 and seen[e].get(c, 0) < counters[c]:
                        streams[e].append(("wait", c, counters[c]))
        self.n_sems = len(chan_names)
        with ExitStack() as es:
            sems = {}
            for c in chan_names:
                sems[c] = (sem_es or es).enter_context(nc.semaphore(prefix + "s_%s_%s" % c))
            block = es.enter_context(nc.Block())

            def run(engname, eng):
                for s in streams[engname]:
                    if s[0] == "wait":
                        eng.wait_ge(sems[s[1]], s[2])
                    else:
                        o = ops[s[1]]
                        ins = o["fn"](eng)
                        if o["sig"]:
                            ins.then_inc(sems[o["chan"]], o["inc"] if o["dma"] is not None else 1)

            @block.tensor
            def _(eng):
                run("pe", eng)

            @block.scalar
            def _(eng):
                run("act", eng)

            @block.vector
            def _(eng):
                run("dve", eng)

            @block.gpsimd
            def _(eng):
                run("pool", eng)

            @block.sync
            def _(eng):
                run("sp", eng)


class Ctx:
    pass


def alloc_common(nc, es, c):
    pfx = getattr(c, "pfx", "")
    sb = lambda name, shape, dt: es.enter_context(nc.sbuf_tensor(pfx + "sb_" + name, shape, dt))
    if not hasattr(c, "hT"):
        c.hT = sb("hT", [128, NC8, TOK], F32)
    c.nT = sb("nT", [128, NC8, TOK], BF16)
    c.aT = sb("aT", [128, 6, TOK], BF16)
    c.sq = sb("sq", [128, NC8, 512], BF16)
    c.rs = sb("rs", [128, 512], F32)
    c.rstd = sb("rstd", [128, 512], F32)
    c.ones = sb("ones", [128, 128], BF16)
    c.U = [sb("U%d" % i, [128, 2, 516], F32) for i in range(2)]
    c.C = [sb("C%d" % i, [128, 2, 512], F32) for i in range(2)]
    c.wup = [sb("wup%d" % i, [128, 2, NC8, 256], BF16) for i in range(2)]
    c.wd = [sb("wd%d" % i, [128, 6, 256], BF16) for i in range(2)]
    if not getattr(c, "no_wsq", False):
        c.wsq = [sb("wsq%d" % i, [128, NC8, 128], BF16) for i in range(2)]
    c.cw = sb("cw", [128, 44, 3], F32)
    c.cb = sb("cb", [128, 44], F32)
    c.gains = sb("gains", [128, 4, NC8], F32)
    if not hasattr(c, "ps"):
        c.ps = es.enter_context(nc.psum_tensor("ps", [128, 8, 512], F32))
    c.eps = sb("epsb", [128, 1], F32)


def emit_consts(P, c):
    P.op("pool", lambda e: e.memset(c.ones[:], 1.0), writes=["ones"])
    P.op("pool", lambda e: e.memset(c.eps[:], EPS), writes=["eps"])
    for i in range(2):
        P.op("pool", lambda e, i=i: e.memset(c.U[i][:, :, 0:2], 0.0), writes=[("U", i)])


def emit_rstd(P, c, ti):
    t0, n = TILES[ti]
    P.op("act", lambda e: e.activation(out=c.sq[:, :, 0:n], in_=c.hT[:, :, t0:t0 + n], func=AF.Square),
         reads=[("h", ti)], writes=["sq"])
    for k in range(NC8):
        P.op("pe", lambda e, k=k: e.matmul(c.ps[:, 0, 0:n], lhsT=c.ones[:], rhs=c.sq[:, k, 0:n],
                                          start=(k == 0), stop=(k == NC8 - 1)),
             reads=["sq", "ones"], writes=[("ps", 0)])
    P.op("act", lambda e: e.activation(out=c.rs[:, 0:n], in_=c.ps[:, 0, 0:n], func=AF.Sqrt,
                                       bias=c.eps[:], scale=1.0 / D),
         reads=[("ps", 0), "eps"], writes=["rs"])
    P.op("dve", lambda e: e.reciprocal(out=c.rstd[:, 0:n], in_=c.rs[:, 0:n]), reads=["rs"], writes=["rstd"])


def emit_norm_bf16(P, c, ti, gidx, dst, dkey):
    t0, n = TILES[ti]
    for k in range(NC8):
        P.op("dve", lambda e, k=k: e.scalar_tensor_tensor(
            out=dst[:, k, t0:t0 + n], in0=c.hT[:, k, t0:t0 + n], scalar=c.gains[:, gidx, k:k + 1],
            in1=c.rstd[:, 0:n], op0=ALU.mult, op1=ALU.mult),
            reads=[("h", ti), "rstd", "gains"], writes=[(dkey, ti)])


def emit_ffn(P, c, w_up, w_down):
    blocks = []
    for gi, grp in enumerate(GROUPS):
        for b0 in range(0, len(grp), 2):
            blocks.append((gi, grp[b0:b0 + 2], b0))

    def load_up(bi):
        _, js, _ = blocks[bi]
        wb = bi % 2
        nbk = len(js)
        for half, col in ((0, js[0] * 128), (1, DFF + js[0] * 128)):
            P.op("pool", lambda e, wb=wb, half=half, col=col, nbk=nbk: e.dma_start(
                out=c.wup[wb][:, half, :, 0:nbk * 128],
                in_=w_up[:, col:col + nbk * 128].rearrange("(k p) f -> p k f", p=128)),
                writes=[("wup", wb)], dma="wup%d" % wb)

    def load_down(gi, dq):
        grp = GROUPS[gi]
        ng = len(grp)
        r0 = grp[0] * 128
        db = dq % 2
        P.op("pool", lambda e: e.dma_start(
            out=c.wd[db][:, 0:ng, :],
            in_=w_down[r0:r0 + ng * 128, dq * 256:(dq + 1) * 256].rearrange("(j p) d -> p j d", p=128)),
            writes=[("wd", db)], dma="wd%d" % db)

    uctr = 0
    load_up(0)
    bi = 0
    for gi, grp in enumerate(GROUPS):
        gblocks = [blk for blk in blocks if blk[0] == gi]
        for bidx, (_, js, b0) in enumerate(gblocks):
            if bi + 1 < len(blocks):
                load_up(bi + 1)
            if bidx == 1:
                load_down(gi, 0)
                load_down(gi, 1)
            wb = bi % 2
            bi += 1
            for jo, j in enumerate(js):
                jj = b0 + jo
                for ti, (t0, n) in enumerate(TILES):
                    ub = uctr % 2
                    uctr += 1
                    for half in range(2):
                        bank = 1 + 2 * ub + half
                        for k in range(NC8):
                            P.op("pe", lambda e, half=half, k=k, bank=bank, wb=wb, t0=t0, n=n, jo=jo: e.matmul(
                                c.ps[:, bank, 0:n], lhsT=c.wup[wb][:, half, k, jo * 128:(jo + 1) * 128],
                                rhs=c.nT[:, k, t0:t0 + n], start=(k == 0), stop=(k == NC8 - 1)),
                                reads=[("wup", wb), ("n", ti)], writes=[("ps", bank)])
                    for half in range(2):
                        bank = 1 + 2 * ub + half
                        ch = j if half == 0 else NJ + j
                        P.op("act", lambda e, half=half, bank=bank, ub=ub, n=n: e.activation(
                            out=c.U[ub][:, half, 2:2 + n], in_=c.ps[:, bank, 0:n], func=AF.Identity),
                            reads=[("ps", bank)], writes=[("U", ub)])
                        P.op("act", lambda e, half=half, bank=bank, ub=ub, n=n, ch=ch: e.activation(
                            out=c.C[ub][:, half, 0:n], in_=c.ps[:, bank, 0:n], func=AF.Identity,
                            bias=c.cb[:, ch:ch + 1], scale=c.cw[:, ch, 2:3]),
                            reads=[("ps", bank), "cw"], writes=[("C", ub)])
                    for half in range(2):
                        ch = j if half == 0 else NJ + j
                        for sh in (1, 0):
                            P.op("dve", lambda e, half=half, ub=ub, n=n, ch=ch, sh=sh: e.scalar_tensor_tensor(
                                out=c.C[ub][:, half, 0:n], in0=c.U[ub][:, half, sh:sh + n],
                                scalar=c.cw[:, ch, sh:sh + 1], in1=c.C[ub][:, half, 0:n],
                                op0=ALU.mult, op1=ALU.add),
                                reads=[("U", ub), ("C", ub), "cw"], writes=[("C", ub)])
                    nb = (ub + 1) % 2
                    if ti + 1 < len(TILES):
                        P.op("pool", lambda e, ub=ub, nb=nb, n=n: e.tensor_copy(
                            out=c.U[nb][:, :, 0:2], in_=c.U[ub][:, :, n:n + 2]),
                            reads=[("U", ub)], writes=[("U", nb)])
                    else:
                        P.op("pool", lambda e, nb=nb: e.memset(c.U[nb][:, :, 0:2], 0.0), writes=[("U", nb)])
                    P.op("act", lambda e, ub=ub, n=n: e.activation(
                        out=c.U[ub][:, 0, 2:2 + n], in_=c.C[ub][:, 0, 0:n], func=AF.Silu),
                        reads=[("C", ub)], writes=[("U", ub)])
                    P.op("dve", lambda e, ub=ub, n=n, jj=jj, t0=t0: e.tensor_tensor(
                        out=c.aT[:, jj, t0:t0 + n], in0=c.U[ub][:, 0, 2:2 + n], in1=c.C[ub][:, 1, 0:n], op=ALU.mult),
                        reads=[("U", ub), ("C", ub)], writes=[("a", ti)])
        if len(gblocks) < 2:
            load_down(gi, 0)
            load_down(gi, 1)
        ng = len(grp)
        for dc in range(NC8):
            dq = dc // 2
            dh = dq % 2
            for ti, (t0, n) in enumerate(TILES):
                bank = 5 + (ti % 2)
                for jj in range(ng):
                    P.op("pe", lambda e, jj=jj, bank=bank, dh=dh, dc=dc, t0=t0, n=n, ng=ng: e.matmul(
                        c.ps[:, bank, 0:n], lhsT=c.wd[dh][:, jj, (dc % 2) * 128:(dc % 2 + 1) * 128],
                        rhs=c.aT[:, jj, t0:t0 + n], start=(jj == 0), stop=(jj == ng - 1)),
                        reads=[("wd", dh), ("a", ti)], writes=[("ps", bank)])
                P.op("dve", lambda e, bank=bank, dc=dc, t0=t0, n=n: e.tensor_tensor(
                    out=c.hT[:, dc, t0:t0 + n], in0=c.ps[:, bank, 0:n], in1=c.hT[:, dc, t0:t0 + n], op=ALU.add),
                    reads=[("ps", bank), ("h", ti)], writes=[("h", ti)])
            if dc % 2 == 1 and dq + 2 < 4:
                load_down(gi, dq + 2)


def emit_proj_fm(P, c, w, col0, nchunks, ti_list, consume, srcT=None, skey="n", tag="wsq"):
    srcT = c.nT if srcT is None else srcT
    for oc in range(nchunks):
        wb = c.wsq_ctr % 2
        c.wsq_ctr += 1
        col = col0 + oc * 128
        P.op("pool", lambda e, wb=wb, col=col: e.dma_start(
            out=c.wsq[wb][:], in_=w[:, col:col + 128].rearrange("(k p) f -> p k f", p=128)),
            writes=[("wsq", wb)], dma="wsq%d" % wb)
        for ti in ti_list:
            t0, n = TILES[ti]
            bank = 5 + (c.pbank_ctr % 2)
            c.pbank_ctr += 1
            for k in range(NC8):
                P.op("pe", lambda e, k=k, bank=bank, wb=wb, t0=t0, n=n: e.matmul(
                    c.ps[:, bank, 0:n], lhsT=c.wsq[wb][:, k, :], rhs=srcT[:, k, t0:t0 + n],
                    start=(k == 0), stop=(k == NC8 - 1)),
                    reads=[("wsq", wb), (skey, ti)], writes=[("ps", bank)])
            consume(oc, ti, bank)


def l1_front(nc, es, c, P, io):
    xT, gains_d, pscale_d, poolw_d, cwtab_d = io["xT"], io["gains"], io["pscale"], io["poolw"], io["cwtab"]
    w_up, cw_d, cb_d, w_down = io["w_up"], io["cw"], io["cb"], io["w_down"]
    pfx = getattr(c, "pfx", "")
    if True:
        sb = lambda name, shape, dtp: es.enter_context(nc.sbuf_tensor(pfx + "sb_" + name, shape, dtp))
        nbuf = sb("nbuf", [128, NC8, 16 + 512], F32)
        sA = sb("sA", [128, 16 + 512], F32)
        sB = sb("sB", [128, 16 + 512], F32)
        poolw = sb("poolw", [128, 4, 2, 256], BF16)
        pscale = sb("pscale", [128, NC8], F32)
        cwtab = sb("cwtab", [128, 4, 16], F32)
        tmp16 = sb("tmp16", [128, 16], F32)
        emit_consts(P, c)
        P.op("sp", lambda e: e.dma_start(out=c.gains[:], in_=gains_d), writes=["gains"], dma="c0")
        P.op("sp", lambda e: e.dma_start(out=pscale[:], in_=pscale_d), writes=["pscale"], dma="c1")
        P.op("sp", lambda e: e.dma_start(out=cwtab[:], in_=cwtab_d), writes=["cwtab"], dma="c2")
        P.op("sp", lambda e: e.dma_start(out=c.cw[:], in_=cw_d), writes=["cw"], dma="c3")
        P.op("sp", lambda e: e.dma_start(out=c.cb[:], in_=cb_d), writes=["cw"], dma="c3")
        P.op("pool", lambda e: e.dma_start(out=poolw[:], in_=poolw_d.rearrange("g (k p) d -> p g k d", p=128)),
             writes=["poolw"], dma="c4")
        for ti, (t0, n) in enumerate(TILES):
            P.op("sp", lambda e, t0=t0, n=n: e.dma_start(
                out=c.hT[:, :, t0:t0 + n], in_=xT[:, t0:t0 + n].rearrange("(k p) t -> p k t", p=128)),
                writes=[("h", ti)], dma="x%d" % ti)
        P.op("pool", lambda e: e.memset(nbuf[:, :, 0:16], 0.0), writes=["nbuf"])

        for ti, (t0, n) in enumerate(TILES):
            emit_rstd(P, c, ti)
            for k in range(NC8):
                P.op("dve", lambda e, k=k, t0=t0, n=n: e.scalar_tensor_tensor(
                    out=nbuf[:, k, 16:16 + n], in0=c.hT[:, k, t0:t0 + n], scalar=c.gains[:, 0, k:k + 1],
                    in1=c.rstd[:, 0:n], op0=ALU.mult, op1=ALU.mult),
                    reads=[("h", ti), "rstd", "gains"], writes=["nbuf"])
            for k in range(NC8):
                g = k // 2
                w = POOLW[g]
                src = nbuf[:, k, :]
                cur = None
                lvl = 1
                bufs = [sA, sB]
                bi = 0
                while lvl < w:
                    lo = 16 - (w - 2 * lvl) if (w - 2 * lvl) > 0 else 16
                    dst = bufs[bi]
                    prev = src if cur is None else cur
                    P.op("dve", lambda e, dst=dst, prev=prev, lo=lo, lvl=lvl, n=n: e.tensor_tensor(
                        out=dst[:, lo:16 + n], in0=prev[:, lo:16 + n], in1=prev[:, lo - lvl:16 + n - lvl], op=ALU.add),
                        reads=["nbuf", "sA", "sB"], writes=["sA" if bi == 0 else "sB"])
                    cur = dst
                    bi ^= 1
                    lvl *= 2
                P.op("dve", lambda e, cur=cur, k=k, w=w, t0=t0, n=n: e.scalar_tensor_tensor(
                    out=c.nT[:, k, t0:t0 + n], in0=cur[:, 16:16 + n], scalar=1.0 / w, in1=nbuf[:, k, 16:16 + n],
                    op0=ALU.mult, op1=ALU.subtract),
                    reads=["nbuf", "sA", "sB"], writes=[("n", ti)])
                if ti == 0:
                    P.op("dve", lambda e, cur=cur, g=g: e.tensor_tensor(
                        out=tmp16[:], in0=cur[:, 16:32], in1=cwtab[:, g, :], op=ALU.mult),
                        reads=["sA", "sB", "cwtab"], writes=["tmp16"])
                    P.op("dve", lambda e, k=k: e.tensor_tensor(
                        out=c.nT[:, k, 0:16], in0=tmp16[:], in1=nbuf[:, k, 16:32], op=ALU.subtract),
                        reads=["tmp16", "nbuf"], writes=[("n", ti)])
            if ti + 1 < len(TILES):
                P.op("pool", lambda e, n=n: e.tensor_copy(out=nbuf[:, :, 0:16], in_=nbuf[:, :, n:n + 16]),
                     reads=["nbuf"], writes=["nbuf"])
            for oc in range(NC8):
                g = oc // 2
                bank = 5 + (oc % 2)
                for kk in range(2):
                    P.op("pe", lambda e, g=g, kk=kk, oc=oc, bank=bank, t0=t0, n=n: e.matmul(
                        c.ps[:, bank, 0:n], lhsT=poolw[:, g, kk, (oc % 2) * 128:(oc % 2) * 128 + 128],
                        rhs=c.nT[:, 2 * g + kk, t0:t0 + n], start=(kk == 0), stop=(kk == 1)),
                        reads=["poolw", ("n", ti)], writes=[("ps", bank)])
                P.op("dve", lambda e, oc=oc, bank=bank, t0=t0, n=n: e.scalar_tensor_tensor(
                    out=c.hT[:, oc, t0:t0 + n], in0=c.ps[:, bank, 0:n], scalar=pscale[:, oc:oc + 1],
                    in1=c.hT[:, oc, t0:t0 + n], op0=ALU.mult, op1=ALU.add),
                    reads=[("ps", bank), ("h", ti), "pscale"], writes=[("h", ti)])
        for ti in range(len(TILES)):
            emit_rstd(P, c, ti)
            emit_norm_bf16(P, c, ti, 1, c.nT, "n")
        emit_ffn(P, c, w_up, w_down)


def build_l1():
    nc = bass.Bass("TRN2", target_bir_lowering=False)
    dt = lambda name, shape, dtype, kind: nc.dram_tensor(name, shape, dtype, kind=kind).ap()
    xT = dt("xT", [D, TOK], F32, "ExternalInput")
    gains_d = dt("gains", [128, 4, NC8], F32, "ExternalInput")
    pscale_d = dt("pscale", [128, NC8], F32, "ExternalInput")
    poolw_d = dt("poolw", [4, 256, 256], F32, "ExternalInput")
    cwtab_d = dt("cwtab", [128, 4, 16], F32, "ExternalInput")
    w_up = dt("w_up", [D, 2 * DFF], F32, "ExternalInput")
    cw_d = dt("cw", [128, 44, 3], F32, "ExternalInput")
    cb_d = dt("cb", [128, 44], F32, "ExternalInput")
    w_down = dt("w_down", [DFF, D], F32, "ExternalInput")
    w_q = dt("w_q", [D, D], F32, "ExternalInput")
    w_kv = dt("w_kv", [D, 2 * D], F32, "ExternalInput")
    h0T = dt("h0T", [D, TOK], F32, "ExternalOutput")
    qT = dt("qT", [D, TOK], BF16, "ExternalOutput")
    kT = dt("kT", [D, TOK], BF16, "ExternalOutput")
    vv = dt("v", [TOK, D], BF16, "ExternalOutput")

    with ExitStack() as es:
        c = Ctx()
        alloc_common(nc, es, c)
        c.wsq_ctr = 0
        c.pbank_ctr = 0
        P = Prog(nc)
        io = dict(xT=xT, gains=gains_d, pscale=pscale_d, poolw=poolw_d, cwtab=cwtab_d, w_up=w_up, cw=cw_d, cb=cb_d, w_down=w_down)
        l1_front(nc, es, c, P, io)
        sb = lambda name, shape, dtp: es.enter_context(nc.sbuf_tensor("sb_" + name, shape, dtp))
        wv = sb("wv", [128, NC8, 512], BF16)
        stg = [sb("stg%d" % i, [128, 512], BF16) for i in range(2)]
        vstg = stg
        for ti, (t0, n) in enumerate(TILES):
            P.op("sp", lambda e, t0=t0, n=n: e.dma_start(
                out=h0T[:, t0:t0 + n].rearrange("(k p) t -> p k t", p=128), in_=c.hT[:, :, t0:t0 + n]),
                reads=[("h", ti)], writes=["h0T"], dma="oh")
        sctr = [0]

        def evac_to(dst, scale):
            def consume(oc, ti, bank):
                t0, n = TILES[ti]
                s = sctr[0] % 2
                sctr[0] += 1
                P.op("act", lambda e: e.activation(out=stg[s][:, 0:n], in_=c.ps[:, bank, 0:n], func=AF.Identity, scale=scale),
                     reads=[("ps", bank)], writes=[("stg", s)])
                P.op("sp", lambda e: e.dma_start(out=dst[oc * 128:(oc + 1) * 128, t0:t0 + n], in_=stg[s][:, 0:n]),
                     reads=[("stg", s)], writes=["oq%d" % s], dma="stg%d" % s)
            return consume

        all_t = list(range(len(TILES)))
        for ti in all_t:
            emit_rstd(P, c, ti)
            emit_norm_bf16(P, c, ti, 3, c.nT, "n")
        emit_proj_fm(P, c, w_q, 0, NC8, all_t, evac_to(qT, 0.125))
        for ti in all_t:
            emit_rstd(P, c, ti)
            emit_norm_bf16(P, c, ti, 2, c.nT, "n")
        emit_proj_fm(P, c, w_kv, 0, NC8, all_t, evac_to(kT, 1.0))
        vctr = 0
        for hf in range(2):
            P.op("pool", lambda e, hf=hf: e.dma_start(
                out=wv[:], in_=w_kv[:, D + hf * 512:D + (hf + 1) * 512].rearrange("(k p) f -> p k f", p=128)),
                writes=["wv"], dma="wv")
            for ti, (t0, n) in enumerate(TILES):
                for b0 in range(0, n, 128):
                    s = vctr % 2
                    vctr += 1
                    bank = 5 + s
                    for k in range(NC8):
                        P.op("pe", lambda e, k=k, bank=bank, t0=t0, b0=b0: e.matmul(
                            c.ps[:, bank, :], lhsT=c.nT[:, k, t0 + b0:t0 + b0 + 128], rhs=wv[:, k, :],
                            start=(k == 0), stop=(k == NC8 - 1)),
                            reads=["wv", ("n", ti)], writes=[("ps", bank)])
                    P.op("act", lambda e, bank=bank, s=s: e.activation(
                        out=vstg[s][:], in_=c.ps[:, bank, :], func=AF.Identity),
                        reads=[("ps", bank)], writes=[("stg", s)])
                    P.op("sp", lambda e, s=s, t0=t0, b0=b0, hf=hf: e.dma_start(
                        out=vv[t0 + b0:t0 + b0 + 128, hf * 512:(hf + 1) * 512], in_=vstg[s][:]),
                        reads=[("stg", s)], writes=["oq%d" % s], dma="stg%d" % s)
        P.build(final_keys=["h0T", "oq0", "oq1"])
        c.P = P
    return nc


def l3_body(nc, es, c, P, io, load_o=None):
    h0T, oT, gains_d, w_o, w_up, cw_d, cb_d, w_down, outT = (io.get(k) for k in (
        "h0T", "oT", "gains", "w_o", "w_up", "cw", "cb", "w_down", "outT"))
    emit_consts(P, c)
    P.op("sp", lambda e: e.dma_start(out=c.gains[:], in_=gains_d), writes=["gains"], dma="c0")
    P.op("sp", lambda e: e.dma_start(out=c.cw[:], in_=cw_d), writes=["cw"], dma="c3")
    P.op("sp", lambda e: e.dma_start(out=c.cb[:], in_=cb_d), writes=["cw"], dma="c3")
    if load_o is None:
        for ti, (t0, n) in enumerate(TILES):
            P.op("sp", lambda e, t0=t0, n=n: e.dma_start(
                out=c.hT[:, :, t0:t0 + n], in_=h0T[:, t0:t0 + n].rearrange("(k p) t -> p k t", p=128)),
                writes=[("h", ti)], dma="x%d" % ti)
            P.op("sp", lambda e, t0=t0, n=n: e.dma_start(
                out=c.nT[:, :, t0:t0 + n], in_=oT[:, t0:t0 + n].rearrange("(k p) t -> p k t", p=128)),
                writes=[("n", ti)], dma="o%d" % ti)
    else:
        load_o()
    all_t = list(range(len(TILES)))

    def add_to_h(oc, ti, bank):
        t0, n = TILES[ti]
        P.op("dve", lambda e: e.tensor_tensor(
            out=c.hT[:, oc, t0:t0 + n], in0=c.ps[:, bank, 0:n], in1=c.hT[:, oc, t0:t0 + n], op=ALU.add),
            reads=[("ps", bank), ("h", ti)], writes=[("h", ti)])

    emit_proj_fm(P, c, w_o, 0, NC8, all_t, add_to_h)
    for ti in all_t:
        emit_rstd(P, c, ti)
        emit_norm_bf16(P, c, ti, 0, c.nT, "n")
    emit_ffn(P, c, w_up, w_down)
    octr = 0
    for ti, (t0, n) in enumerate(TILES):
        emit_rstd(P, c, ti)
        for k in range(NC8):
            s = octr % 4
            octr += 1
            stg = c.C[s // 2][:, s % 2, :]
            P.op("dve", lambda e, k=k, stg=stg, t0=t0, n=n: e.scalar_tensor_tensor(
                out=stg[:, 0:n], in0=c.hT[:, k, t0:t0 + n], scalar=c.gains[:, 1, k:k + 1],
                in1=c.rstd[:, 0:n], op0=ALU.mult, op1=ALU.mult),
                reads=[("h", ti), "rstd", "gains"], writes=[("ostg", s)])
            P.op("sp", lambda e, k=k, stg=stg, t0=t0, n=n: e.dma_start(
                out=outT[k * 128:(k + 1) * 128, t0:t0 + n], in_=stg[:, 0:n]),
                reads=[("ostg", s)], writes=["out%d" % s], dma="ostg%d" % s)


def build_l3():
    nc = bass.Bass("TRN2", target_bir_lowering=False)
    dt = lambda name, shape, dtype, kind: nc.dram_tensor(name, shape, dtype, kind=kind).ap()
    h0T = dt("h0T", [D, TOK], F32, "ExternalInput")
    oT = dt("oT", [D, TOK], BF16, "ExternalInput")
    gains_d = dt("gains", [128, 4, NC8], F32, "ExternalInput")
    w_o = dt("w_o", [D, D], F32, "ExternalInput")
    w_up = dt("w_up", [D, 2 * DFF], F32, "ExternalInput")
    cw_d = dt("cw", [128, 44, 3], F32, "ExternalInput")
    cb_d = dt("cb", [128, 44], F32, "ExternalInput")
    w_down = dt("w_down", [DFF, D], F32, "ExternalInput")
    outT = dt("outT", [D, TOK], F32, "ExternalOutput")
    with ExitStack() as es:
        c = Ctx()
        alloc_common(nc, es, c)
        c.wsq_ctr = 0
        c.pbank_ctr = 0
        P = Prog(nc)
        io = dict(h0T=h0T, oT=oT, gains=gains_d, w_o=w_o, w_up=w_up, cw=cw_d, cb=cb_d, w_down=w_down, outT=outT)
        l3_body(nc, es, c, P, io)
        P.build(final_keys=["out0", "out1", "out2", "out3"])
        c.P = P
    return nc


def l3_inputs(inp, h0T_list, oT_list):
    maps = []
    gains = np.zeros((128, 4, NC8), np.float32)
    gains[:, 0] = lay_gain(inp["ffn_norm"][1])
    gains[:, 1] = lay_gain(inp["final_norm"])
    cw = np.ascontiguousarray(inp["ffn_conv_w"][1].T.reshape(44, 128, 3).transpose(1, 0, 2))
    cb = np.ascontiguousarray(inp["ffn_conv_b"][1].reshape(44, 128).T)
    for ci in range(len(h0T_list)):
        maps.append({
            "h0T": h0T_list[ci], "oT": oT_list[ci], "gains": gains,
            "w_o": np.ascontiguousarray(inp["w_o"][0]),
            "w_up": np.ascontiguousarray(inp["ffn_w_up"][1]),
            "cw": cw, "cb": cb,
            "w_down": np.ascontiguousarray(inp["ffn_w_down"][1]),
        })
    return maps


QTILES = [(i * 512, 512) for i in range(8)] + [(4096, 128)]


def l2_body(nc, es, P, ps, qT, kT, vv, cst_d, nmask_d, emit_out, pfx=""):
    sb = lambda name, shape, dtp: es.enter_context(nc.sbuf_tensor(pfx + "sb_" + name, shape, dtp))
    qs = [sb("q%d" % i, [128, LPAD], BF16) for i in range(2)]
    ks = [sb("k%d" % i, [128, LPAD], BF16) for i in range(2)]
    vs = [sb("v%d" % i, [128, NBLK, 128], BF16) for i in range(2)]
    osb = sb("osb", [128, 4, LPAD], BF16)
    E = [sb("E%d" % i, [128, 2, 512], F32) for i in range(3)]
    SP = [sb("SP%d" % i, [128, 2, 512], BF16) for i in range(3)]
    A = [sb("A%d" % i, [128, 2, 512], BF16) for i in range(3)]
    Rb = [sb("R%d" % i, [128, 2, 512], BF16) for i in range(3)]
    cst = sb("cst", [128, 3, 128], BF16)
    nmask = sb("nmask", [128, 4, 512], BF16)
    onec = sb("onec", [128, 1], F32)
    if ps is None:
        ps = es.enter_context(nc.psum_tensor("ps", [128, 8, 512], F32))
    P.op("pool", lambda e: e.memset(onec[:], 1.0), writes=["onec"])
    P.op("sp", lambda e: e.dma_start(out=cst[:], in_=cst_d), writes=["cst"], dma="c0")
    P.op("sp", lambda e: e.dma_start(out=nmask[:], in_=nmask_d), writes=["nmask"], dma="c1")
    units = []
    for hp in range(4):
        for qi, (q0, nq) in enumerate(QTILES):
            kmax = (q0 + nq) // 128 - 1
            kdiag0 = q0 // 128
            for i, kb in enumerate(range(kmax, -1, -1)):
                units.append(dict(hp=hp, qi=qi, q0=q0, nq=nq, kb=kb, i=i, last=(kb == 0),
                                  r=(kb - kdiag0) if kb >= kdiag0 else None))
    loaded = set()

    def ensure_loaded(hp):
        if hp in loaded or hp >= 4:
            return
        loaded.add(hp)
        b = hp % 2
        P.op("sp", lambda e: e.dma_start(out=qs[b][:], in_=qT[hp * 128:(hp + 1) * 128, :]),
             reads=["qT_d0", "qT_d1"], writes=[("q", b)], dma="q%d" % b)
        P.op("sp", lambda e: e.dma_start(out=ks[b][:], in_=kT[hp * 128:(hp + 1) * 128, :]),
             reads=["kT_d0", "kT_d1"], writes=[("k", b)], dma="k%d" % b)
        P.op("sp", lambda e: e.dma_start(
            out=vs[b][:], in_=vv[:, hp * 128:(hp + 1) * 128].rearrange("(k p) f -> p k f", p=128)),
            reads=["v_d0", "v_d1"], writes=[("v", b)], dma="v%d" % b)

    def s1a(u, idx):
        b = u["hp"] % 2
        zb = 2 * (idx % 3)
        q0, nq, kb = u["q0"], u["nq"], u["kb"]
        for h in range(2):
            P.op("pe", lambda e, h=h: e.matmul(
                ps[:, zb + h, 0:nq], lhsT=ks[b][64 * h:64 * h + 64, kb * 128:(kb + 1) * 128],
                rhs=qs[b][64 * h:64 * h + 64, q0:q0 + nq], start=True, stop=(u["r"] is None)),
                reads=[("q", b), ("k", b)], writes=[("z", idx % 3)])
            if u["r"] is not None:
                P.op("pe", lambda e, h=h: e.matmul(
                    ps[:, zb + h, 0:nq], lhsT=cst[:, 2, :], rhs=nmask[:, u["r"], 0:nq], start=False, stop=True),
                    reads=["cst", "nmask"], writes=[("z", idx % 3)])
        P.op("act", lambda e: e.activation(out=E[idx % 3][:, :, 0:nq], in_=ps[:, zb:zb + 2, 0:nq], func=AF.Exp),
             reads=[("z", idx % 3)], writes=[("E", idx % 3)])

    def s1b(u, idx):
        nq = u["nq"]
        P.op("act", lambda e: e.activation(out=SP[idx % 3][:, :, 0:nq], in_=E[idx % 3][:, :, 0:nq],
                                           func=AF.Ln, bias=onec[:]),
             reads=[("E", idx % 3), "onec"], writes=[("SP", idx % 3)])
        if not u["last"]:
            i = u["i"]
            if i == 0:
                P.op("pool", lambda e: e.tensor_copy(out=Rb[(idx + 1) % 3][:, :, 0:nq], in_=SP[idx % 3][:, :, 0:nq]),
                     reads=[("SP", idx % 3)], writes=[("R", (idx + 1) % 3)])
            else:
                P.op("pool", lambda e: e.tensor_tensor(
                    out=Rb[(idx + 1) % 3][:, :, 0:nq], in0=Rb[idx % 3][:, :, 0:nq], in1=SP[idx % 3][:, :, 0:nq], op=ALU.add),
                    reads=[("SP", idx % 3), ("R", idx % 3)], writes=[("R", (idx + 1) % 3)])

    def s2(u, idx):
        zb = 2 * (idx % 3)
        nq, i = u["nq"], u["i"]
        for h in range(2):
            P.op("pe", lambda e, h=h: e.matmul(
                ps[:, zb + h, 0:nq], lhsT=cst[:, 0, :], rhs=SP[idx % 3][:, h, 0:nq], start=False, stop=(i == 0)),
                reads=["cst", ("SP", idx % 3), ("E", idx % 3)], writes=[("z", idx % 3)])
            if i > 0:
                P.op("pe", lambda e, h=h: e.matmul(
                    ps[:, zb + h, 0:nq], lhsT=cst[:, 1, :], rhs=Rb[idx % 3][:, h, 0:nq], start=False, stop=True),
                    reads=["cst", ("R", idx % 3)], writes=[("z", idx % 3)])
        P.op("act", lambda e: e.activation(out=A[idx % 3][:, :, 0:nq], in_=ps[:, zb:zb + 2, 0:nq], func=AF.Exp),
             reads=[("z", idx % 3)], writes=[("A", idx % 3)])

    def s3(u, idx):
        b = u["hp"] % 2
        ob = 6 + (u["qi"] % 2)
        q0, nq, kb, i, hp = u["q0"], u["nq"], u["kb"], u["i"], u["hp"]
        for h in range(2):
            P.op("pe", lambda e, h=h: e.matmul(
                ps[64 * h:64 * h + 64, ob, 0:nq], lhsT=vs[b][:, kb, 64 * h:64 * h + 64],
                rhs=A[idx % 3][:, h, 0:nq], start=(i == 0), stop=u["last"]),
                reads=[("v", b), ("A", idx % 3)], writes=[("o", ob)])
        if u["last"]:
            P.op("dve", lambda e: e.tensor_copy(out=osb[:, hp, q0:q0 + nq], in_=ps[:, ob, 0:nq]),
                 reads=[("o", ob)], writes=[("osb", hp)])
            if u["qi"] == len(QTILES) - 1:
                emit_out(hp, osb)

    n = len(units)
    ensure_loaded(0)
    for idx in range(n + 3):
        if idx < n:
            u = units[idx]
            ensure_loaded(u["hp"])
            if u["qi"] == 4 and u["i"] == 0:
                ensure_loaded(u["hp"] + 1)
            s1a(u, idx)
        if 0 <= idx - 2 < n:
            s2(units[idx - 2], idx - 2)
        if idx < n:
            s1b(units[idx], idx)
        if 0 <= idx - 3 < n:
            s3(units[idx - 3], idx - 3)


def build_l2():
    nc = bass.Bass("TRN2", target_bir_lowering=False)
    dt = lambda name, shape, dtype, kind: nc.dram_tensor(name, shape, dtype, kind=kind).ap()
    qT = dt("qT", [512, LPAD], BF16, "ExternalInput")
    kT = dt("kT", [512, LPAD], BF16, "ExternalInput")
    vv = dt("v", [LPAD, 512], BF16, "ExternalInput")
    cst_d = dt("cst", [128, 3, 128], BF16, "ExternalInput")
    nmask_d = dt("nmask", [128, 4, 512], BF16, "ExternalInput")
    oT = dt("oT", [512, LPAD], BF16, "ExternalOutput")
    with ExitStack() as es:
        P = Prog(nc)

        def emit_out(hp, osb):
            P.op("sp", lambda e: e.dma_start(out=oT[hp * 128:(hp + 1) * 128, :], in_=osb[:, hp, :]),
                 reads=[("osb", hp)], writes=["oT%d" % hp], dma="oT%d" % hp)

        l2_body(nc, es, P, None, qT, kT, vv, cst_d, nmask_d, emit_out)
        P.build(final_keys=["oT0", "oT1", "oT2", "oT3"])
    return nc


def l2_consts():
    j = np.arange(128)
    ntri = np.where(j[:, None] >= j[None, :], -1.0, 0.0)
    cst = np.stack([ntri, -np.ones((128, 128)), np.eye(128)], axis=1).astype(ml_dtypes.bfloat16)
    t = np.arange(512)
    nmask = np.zeros((128, 4, 512), np.float32)
    for r in range(4):
        nmask[:, r, :] = np.where((128 * r + j)[:, None] >= t[None, :], NEG, 0.0)
    return np.ascontiguousarray(cst), nmask.astype(ml_dtypes.bfloat16)


RG = [[0, 1], [2, 3], [4, 5], [6, 7]]


def build_fused():
    nc = bass.Bass("TRN2", target_bir_lowering=False)
    dt = lambda name, shape, dtype, kind="ExternalInput": nc.dram_tensor(name, shape, dtype, kind=kind).ap()
    io1 = dict(xT=dt("xT", [D, TOK], F32), gains=dt("gains1", [128, 4, NC8], F32), pscale=dt("pscale", [128, NC8], F32),
               poolw=dt("poolw", [4, 256, 256], F32), cwtab=dt("cwtab", [128, 4, 16], F32),
               w_up=dt("w_up0", [D, 2 * DFF], F32), cw=dt("cw0", [128, 44, 3], F32), cb=dt("cb0", [128, 44], F32),
               w_down=dt("w_down0", [DFF, D], F32))
    wqkv_d = [dt("wq_hg", [D, 512], F32), dt("wk_hg", [D, 512], F32), dt("wv_hg", [D, 512], F32)]
    cst_d = dt("cst", [128, 3, 128], BF16)
    nmask_d = dt("nmask", [128, 4, 512], BF16)
    sel_d = dt("sel", [128, 2], F32)
    io3 = dict(gains=dt("gains3", [128, 4, NC8], F32), w_o=dt("w_o", [D, D], F32), w_up=dt("w_up1", [D, 2 * DFF], F32),
               cw=dt("cw1", [128, 44, 3], F32), cb=dt("cb1", [128, 44], F32), w_down=dt("w_down1", [DFF, D], F32),
               outT=dt("outT", [D, TOK], F32, "ExternalOutput"))
    snd = [dt("snd%d" % i, [256, TOK], BF16, "Internal") for i in range(2)]
    rcv = [dt("rcv%d" % i, [512, TOK], BF16, "Internal") for i in range(2)]
    nall = [dt("nall%d" % i, [D, LPAD], BF16, "Internal") for i in range(2)]
    qT_d = dt("qT_d", [512, LPAD], BF16, "Internal")
    kT_d = dt("kT_d", [512, LPAD], BF16, "Internal")
    v_d = dt("v_d", [LPAD, 512], BF16, "Internal")
    osnd = [dt("osnd%d" % i, [128, LPAD], BF16, "Internal") for i in range(2)]
    orcv = [dt("orcv%d" % i, [256, LPAD], BF16, "Internal") for i in range(2)]
    oall = dt("oall", [D, LPAD], BF16, "Internal")
    all_t = list(range(len(TILES)))

    def allgather(P, src, dst, skey, dkey):
        P.op("pool", lambda e: e.collective_compute("AllGather", ALU.bypass, replica_groups=RG,
                                                    ins=[src.opt()], outs=[dst.opt()]),
             reads=[skey], writes=[dkey], dma="cc", inc=1)

    with ExitStack() as outer:
        hT = outer.enter_context(nc.sbuf_tensor("sb_hT", [128, NC8, TOK], F32))
        ps = outer.enter_context(nc.psum_tensor("ps", [128, 8, 512], F32))
        with ExitStack() as es:
            c = Ctx()
            c.pfx, c.hT, c.ps = "p1", hT, ps
            c.no_wsq = True
            alloc_common(nc, es, c)
            c.wsq_ctr = 0
            c.pbank_ctr = 0
            P = Prog(nc)
            l1_front(nc, es, c, P, io1)
            cc = 0
            for which, gidx in ((0, 3), (1, 2)):
                for ti in all_t:
                    emit_rstd(P, c, ti)
                    emit_norm_bf16(P, c, ti, gidx, c.nT, "n")
                for j in range(4):
                    b = cc % 2
                    cc += 1
                    for ti, (t0, n) in enumerate(TILES):
                        P.op("sp", lambda e, b=b, j=j, t0=t0, n=n: e.dma_start(
                            out=snd[b].rearrange("(k p) t -> p k t", p=128)[:, :, t0:t0 + n],
                            in_=c.nT[:, 2 * j:2 * j + 2, t0:t0 + n]),
                            reads=[("n", ti)], writes=[("snd", b)], dma="snd%d" % b)
                    allgather(P, snd[b], rcv[b], ("snd", b), ("rcv", b))
                    P.op("sp", lambda e, b=b, j=j, which=which: e.dma_start(
                        out=nall[which][256 * j:256 * j + 256, 0:TOK], in_=rcv[b][0:256, :]),
                        reads=[("rcv", b)], writes=["nall_a%d" % b], dma="na%d" % b)
                    P.op("sp", lambda e, b=b, j=j, which=which: e.dma_start(
                        out=nall[which][256 * j:256 * j + 256, TOK:LPAD], in_=rcv[b][256:512, 128:TOK]),
                        reads=[("rcv", b)], writes=["nall_b%d" % b], dma="nb%d" % b)
            P.build(final_keys=[], sem_es=outer, prefix="p1", barrier=True)
        with ExitStack() as es:
            sb = lambda name, shape, dtp: es.enter_context(nc.sbuf_tensor("p2a_" + name, shape, dtp))
            wsb = [sb("w%d" % i, [128, NC8, 512], BF16) for i in range(3)]
            ntile = [sb("nt%d" % i, [128, 2, NC8, 512], BF16) for i in range(2)]
            stg = [sb("stg%d" % i, [128, 512], BF16) for i in range(2)]
            P = Prog(nc)
            for i in range(3):
                P.op("pool", lambda e, i=i: e.dma_start(out=wsb[i][:], in_=wqkv_d[i].rearrange("(k p) f -> p k f", p=128)),
                     writes=[("w", i)], dma="w%d" % i)
            sctr = 0
            bctr = 0
            for qi, (t0, n) in enumerate(QTILES):
                nb = qi % 2
                for which in range(2):
                    P.op("sp", lambda e, nb=nb, which=which, t0=t0, n=n: e.dma_start(
                        out=ntile[nb][:, which, :, 0:n],
                        in_=nall[which][:, t0:t0 + n].rearrange("(k p) t -> p k t", p=128)),
                        writes=[("nt", nb)], dma="nt%d" % nb)
                for which, dst, scale, key in ((0, qT_d, 0.125, "qT_d"), (1, kT_d, 1.0, "kT_d")):
                    for oc in range(4):
                        bank = 6 + (bctr % 2)
                        bctr += 1
                        for k in range(NC8):
                            P.op("pe", lambda e, which=which, oc=oc, k=k, bank=bank, nb=nb, n=n: e.matmul(
                                ps[:, bank, 0:n], lhsT=wsb[which][:, k, oc * 128:(oc + 1) * 128],
                                rhs=ntile[nb][:, which, k, 0:n], start=(k == 0), stop=(k == NC8 - 1)),
                                reads=[("w", which), ("nt", nb)], writes=[("ps", bank)])
                        s_ = sctr % 2
                        sctr += 1
                        P.op("act", lambda e, s_=s_, bank=bank, n=n, scale=scale: e.activation(
                            out=stg[s_][:, 0:n], in_=ps[:, bank, 0:n], func=AF.Identity, scale=scale),
                            reads=[("ps", bank)], writes=[("stg", s_)])
                        P.op("sp", lambda e, s_=s_, dst=dst, oc=oc, t0=t0, n=n: e.dma_start(
                            out=dst[oc * 128:(oc + 1) * 128, t0:t0 + n], in_=stg[s_][:, 0:n]),
                            reads=[("stg", s_)], writes=["%s%d" % (key, s_)], dma="stg%d" % s_)
                for b0 in range(0, n, 128):
                    bank = 6 + (bctr % 2)
                    bctr += 1
                    for k in range(NC8):
                        P.op("pe", lambda e, k=k, bank=bank, nb=nb, b0=b0: e.matmul(
                            ps[:, bank, :], lhsT=ntile[nb][:, 1, k, b0:b0 + 128], rhs=wsb[2][:, k, :],
                            start=(k == 0), stop=(k == NC8 - 1)),
                            reads=[("w", 2), ("nt", nb)], writes=[("ps", bank)])
                    s_ = sctr % 2
                    sctr += 1
                    P.op("act", lambda e, s_=s_, bank=bank: e.activation(
                        out=stg[s_][:], in_=ps[:, bank, :], func=AF.Identity),
                        reads=[("ps", bank)], writes=[("stg", s_)])
                    P.op("sp", lambda e, s_=s_, t0=t0, b0=b0: e.dma_start(
                        out=v_d[t0 + b0:t0 + b0 + 128, :], in_=stg[s_][:]),
                        reads=[("stg", s_)], writes=["v_d%d" % s_], dma="stg%d" % s_)
            P.build(final_keys=[], sem_es=outer, prefix="p2a", barrier=True)
        with ExitStack() as es:
            P = Prog(nc)

            def emit_out(hp, osb):
                b = hp % 2
                P.op("sp", lambda e: e.dma_start(out=osnd[b], in_=osb[:, hp, :]),
                     reads=[("osb", hp)], writes=[("osnd", b)], dma="osnd%d" % b)
                allgather(P, osnd[b], orcv[b], ("osnd", b), ("orcv", b))
                for s_ in range(2):
                    P.op("sp", lambda e, s_=s_: e.dma_start(
                        out=oall[s_ * 512 + hp * 128:s_ * 512 + (hp + 1) * 128, :], in_=orcv[b][s_ * 128:(s_ + 1) * 128, :]),
                        reads=[("orcv", b)], writes=["oall%d_%d" % (b, s_)], dma="oa%d_%d" % (b, s_))

            l2_body(nc, es, P, ps, qT_d, kT_d, v_d, cst_d, nmask_d, emit_out, pfx="p2b")
            P.build(final_keys=[], sem_es=outer, prefix="p2b", barrier=True)
        with ExitStack() as es:
            c = Ctx()
            c.pfx, c.hT, c.ps = "p3", hT, ps
            alloc_common(nc, es, c)
            c.wsq_ctr = 0
            c.pbank_ctr = 0
            sel = es.enter_context(nc.sbuf_tensor("p3_sel", [128, 2], F32))
            P = Prog(nc)

            def load_o():
                P.op("sp", lambda e: e.dma_start(out=sel[:], in_=sel_d), writes=["sel"], dma="sel")
                for ti, (t0, n) in enumerate(TILES):
                    P.op("sp", lambda e, t0=t0, n=n: e.dma_start(
                        out=c.nT[:, :, t0:t0 + n], in_=oall[:, t0:t0 + n].rearrange("(k p) t -> p k t", p=128)),
                        writes=[("n", ti)], dma="o%d" % ti)
                    P.op("sp", lambda e, t0=t0, n=n: e.dma_start(
                        out=c.sq[:, :, 0:n],
                        in_=oall[:, LPAD - TOK + t0:LPAD - TOK + t0 + n].rearrange("(k p) t -> p k t", p=128)),
                        writes=["sq"], dma="osq")
                    P.op("dve", lambda e, t0=t0, n=n: e.tensor_scalar(
                        out=c.nT[:, :, t0:t0 + n], in0=c.nT[:, :, t0:t0 + n], scalar1=sel[:, 0:1], scalar2=None, op0=ALU.mult),
                        reads=[("n", ti), "sel"], writes=[("n", ti)])
                    P.op("dve", lambda e, t0=t0, n=n: e.scalar_tensor_tensor(
                        out=c.nT[:, :, t0:t0 + n], in0=c.sq[:, :, 0:n], scalar=sel[:, 1:2], in1=c.nT[:, :, t0:t0 + n],
                        op0=ALU.mult, op1=ALU.add),
                        reads=["sq", ("n", ti), "sel"], writes=[("n", ti)])

            l3_body(nc, es, c, P, io3, load_o)
            P.build(final_keys=["out0", "out1", "out2", "out3"], sem_es=outer, prefix="p3")
    return nc


def fused_inputs(inp):
    maps = []
    gains1 = np.stack([lay_gain(inp["mix_norm"][0]), lay_gain(inp["ffn_norm"][0]),
                       lay_gain(inp["kv_norm"]), lay_gain(inp["mix_norm"][1])], axis=1)
    gains3 = np.zeros((128, 4, NC8), np.float32)
    gains3[:, 0] = lay_gain(inp["ffn_norm"][1])
    gains3[:, 1] = lay_gain(inp["final_norm"])
    lay_cw = lambda w: np.ascontiguousarray(w.T.reshape(44, 128, 3).transpose(1, 0, 2))
    lay_cb = lambda b: np.ascontiguousarray(b.reshape(44, 128).T)
    cst, nmask = l2_consts()
    ca = np.ascontiguousarray
    for b in range(4):
        hs = pad_seq(inp["x"][b], inp["meta_tokens"])
        for half in range(2):
            t0 = core_tok0(half)
            cwt = np.zeros((128, 4, 16), np.float32)
            for g, w in enumerate(POOLW):
                for t in range(16):
                    cwt[:, g, t] = 1.0 / (min(w, t + 1) if half == 0 else w)
            sel = np.zeros((128, 2), np.float32)
            sel[:, half] = 1.0
            hg = slice(half * 512, (half + 1) * 512)
            maps.append({
                "xT": ca(hs[t0:t0 + TOK].T), "gains1": ca(gains1), "pscale": lay_gain(inp["pool_scale"][0]),
                "poolw": ca(inp["pool_w"][0]), "cwtab": cwt,
                "w_up0": ca(inp["ffn_w_up"][0]), "cw0": lay_cw(inp["ffn_conv_w"][0]), "cb0": lay_cb(inp["ffn_conv_b"][0]),
                "w_down0": ca(inp["ffn_w_down"][0]),
                "wq_hg": ca(inp["w_q"][0][:, hg]), "wk_hg": ca(inp["w_kv"][:, hg]),
                "wv_hg": ca(inp["w_kv"][:, D + half * 512:D + (half + 1) * 512]),
                "cst": cst, "nmask": nmask, "sel": sel,
                "gains3": gains3, "w_o": ca(inp["w_o"][0]),
                "w_up1": ca(inp["ffn_w_up"][1]), "cw1": lay_cw(inp["ffn_conv_w"][1]), "cb1": lay_cb(inp["ffn_conv_b"][1]),
                "w_down1": ca(inp["ffn_w_down"][1]),
            })
    return maps


def kernel_fused(**inputs):
    inp = {k: np.asarray(v) for k, v in inputs.items()}
    r = run_bass_kernel_spmd(_prog("fused", build_fused), fused_inputs(inp), core_ids=list(range(8))).results
    out = np.empty((4, SEQ, D), np.float32)
    na = TOK - NMETA
    for b in range(4):
        out[b, :na] = r[2 * b]["outT"].T[NMETA:TOK]
        out[b, na:] = r[2 * b + 1]["outT"].T[128:128 + SEQ - na]
    return out


def lay_gain(g):
    return np.ascontiguousarray(g.reshape(NC8, 128).T)


def pad_seq(x_b, meta):
    out = np.zeros((LPAD, D), np.float32)
    out[:NMETA] = meta
    out[NMETA:NMETA + SEQ] = x_b
    return out


def core_tok0(half):
    return 0 if half == 0 else LPAD - TOK


def l1_inputs(inp):
    maps = []
    gains = np.stack([lay_gain(inp["mix_norm"][0]), lay_gain(inp["ffn_norm"][0]),
                      lay_gain(inp["kv_norm"]), lay_gain(inp["mix_norm"][1])], axis=1)
    cw = np.ascontiguousarray(inp["ffn_conv_w"][0].T.reshape(44, 128, 3).transpose(1, 0, 2))
    cb = np.ascontiguousarray(inp["ffn_conv_b"][0].reshape(44, 128).T)
    for b in range(4):
        hs = pad_seq(inp["x"][b], inp["meta_tokens"])
        for half in range(2):
            t0 = core_tok0(half)
            cwt = np.zeros((128, 4, 16), np.float32)
            for g, w in enumerate(POOLW):
                for t in range(16):
                    cwt[:, g, t] = 1.0 / (min(w, t + 1) if half == 0 else w)
            maps.append({
                "xT": np.ascontiguousarray(hs[t0:t0 + TOK].T),
                "gains": np.ascontiguousarray(gains),
                "pscale": lay_gain(inp["pool_scale"][0]),
                "poolw": np.ascontiguousarray(inp["pool_w"][0]),
                "cwtab": cwt,
                "w_up": np.ascontiguousarray(inp["ffn_w_up"][0]),
                "cw": cw, "cb": cb,
                "w_down": np.ascontiguousarray(inp["ffn_w_down"][0]),
                "w_q": np.ascontiguousarray(inp["w_q"][0]),
                "w_kv": np.ascontiguousarray(inp["w_kv"]),
            })
    return maps


_PROGS = {}


def _prog(name, builder):
    if name not in _PROGS:
        _PROGS[name] = builder()
    return _PROGS[name]


def kernel_unfused(**inputs):
    inp = {k: np.asarray(v) for k, v in inputs.items()}
    cores = list(range(8))
    bf = ml_dtypes.bfloat16
    r1 = run_bass_kernel_spmd(_prog("l1", build_l1), l1_inputs(inp), core_ids=cores).results
    cst, nmask = l2_consts()
    maps2 = []
    for b in range(4):
        a, bb = r1[2 * b], r1[2 * b + 1]
        qf = np.concatenate([a["qT"], bb["qT"][:, 128:]], axis=1)
        kf = np.concatenate([a["kT"], bb["kT"][:, 128:]], axis=1)
        vf = np.concatenate([a["v"], bb["v"][128:]], axis=0)
        for hg in range(2):
            sl = slice(hg * 512, (hg + 1) * 512)
            maps2.append({"qT": np.ascontiguousarray(qf[sl]), "kT": np.ascontiguousarray(kf[sl]),
                          "v": np.ascontiguousarray(vf[:, sl]), "cst": cst, "nmask": nmask})
    r2 = run_bass_kernel_spmd(_prog("l2", build_l2), maps2, core_ids=cores).results
    h0T_list, oT_list = [], []
    for b in range(4):
        of = np.concatenate([r2[2 * b]["oT"], r2[2 * b + 1]["oT"]], axis=0)
        for half in range(2):
            t0 = core_tok0(half)
            h0T_list.append(r1[2 * b + half]["h0T"])
            oT_list.append(np.ascontiguousarray(of[:, t0:t0 + TOK]))
    r3 = run_bass_kernel_spmd(_prog("l3", build_l3), l3_inputs(inp, h0T_list, oT_list), core_ids=cores).results
    out = np.empty((4, SEQ, D), np.float32)
    na = TOK - NMETA
    for b in range(4):
        out[b, :na] = r3[2 * b]["outT"].T[NMETA:TOK]
        out[b, na:] = r3[2 * b + 1]["outT"].T[128:128 + SEQ - na]
    return out


def kernel(**inputs):
    return kernel_fused(**inputs)
```
